# Optimizing a Trainium2 kernel written in Bass

```python
import jax, jax.numpy as jnp
from jax import lax
import numpy as np

D_MODEL = 2048
BATCH = 4
SEQ = 2048
DEPTH = 4

ATT_HEADS = 12
ATT_KV_HEADS = 4
HEAD_DIM = 64
WINDOW = 128
ATT_BLOCK = 128
ROPE_THETA = 10000.0
ATT_Q_W = ATT_HEADS * HEAD_DIM
ATT_KV_W = ATT_KV_HEADS * HEAD_DIM
POOL_WINDOWS = (2, 4, 8, 16)
POOL_GROUPS = 4
POOL_GROUP_DIM = 192
POOL_WIDTH = POOL_GROUPS * POOL_GROUP_DIM
GLA_HEADS = 4
GLA_DK = 96
GLA_DV = 192
GLA_QK_W = GLA_HEADS * GLA_DK
GLA_V_W = GLA_HEADS * GLA_DV
GLA_LOWRANK = 16
GLA_TAU = 16.0
GLA_CHUNK = 64
N_BRANCH = 3
D_FF = 5632
EPS = 1e-6
W_IN_COLS = ATT_Q_W + 2 * ATT_KV_W + POOL_WIDTH + 2 * GLA_QK_W + 2 * GLA_V_W + GLA_LOWRANK + N_BRANCH * D_MODEL

kernel_name = "hybrid_gated_swa_pool_gla_macaron"


def rms_norm(x, g):
    xf = x.astype(jnp.float32)
    y = xf * lax.rsqrt(jnp.mean(xf * xf, axis=-1, keepdims=True) + EPS)
    return (y * g.astype(jnp.float32)).astype(x.dtype)


def swiglu(h, wi, wo):
    a, b = jnp.split(h @ wi, 2, axis=-1)
    return (jax.nn.silu(a) * b) @ wo


def rope(x, positions):
    half = HEAD_DIM // 2
    inv_freq = ROPE_THETA ** (-jnp.arange(half, dtype=jnp.float32) / half)
    ang = positions.astype(jnp.float32)[..., None] * inv_freq
    cos = jnp.cos(ang)[:, :, None, :]
    sin = jnp.sin(ang)[:, :, None, :]
    xf = x.astype(jnp.float32)
    x1, x2 = xf[..., :half], xf[..., half:]
    out = jnp.concatenate([x1 * cos - x2 * sin, x2 * cos + x1 * sin], axis=-1)
    return out.astype(x.dtype)


def sliding_window_sink_attention(q, k, v, sinks):
    B, S, H, Dh = q.shape
    nb = S // ATT_BLOCK
    G = H // ATT_KV_HEADS
    qb = q.reshape(B, nb, ATT_BLOCK, ATT_KV_HEADS, G, Dh)
    kb = k.reshape(B, nb, ATT_BLOCK, ATT_KV_HEADS, Dh)
    vb = v.reshape(B, nb, ATT_BLOCK, ATT_KV_HEADS, Dh)
    pad = ((0, 0), (1, 0), (0, 0), (0, 0), (0, 0))
    kw = jnp.concatenate([jnp.pad(kb, pad)[:, :-1], kb], axis=2)
    vw = jnp.concatenate([jnp.pad(vb, pad)[:, :-1], vb], axis=2)
    s = jnp.einsum('bnqkgd,bnskd->bnkgqs', qb, kw, preferred_element_type=jnp.float32) * (HEAD_DIM ** -0.5)
    qi = jnp.arange(ATT_BLOCK)[:, None]
    si = jnp.arange(2 * ATT_BLOCK)[None, :]
    rel = ATT_BLOCK + qi - si
    band = (rel >= 0) & (rel < WINDOW)
    has_prev = (jnp.arange(nb) > 0)[:, None, None] | (si >= ATT_BLOCK)[None]
    mask = band[None] & has_prev
    s = jnp.where(mask[None, :, None, None], s, -jnp.inf)
    sink = sinks.astype(jnp.float32).reshape(ATT_KV_HEADS, G)[None, None, :, :, None, None]
    sink_col = jnp.broadcast_to(sink, s.shape[:-1] + (1,))
    p = jax.nn.softmax(jnp.concatenate([s, sink_col], axis=-1), axis=-1)[..., :-1]
    o = jnp.einsum('bnkgqs,bnskd->bnqkgd', p.astype(v.dtype), vw)
    return o.reshape(B, S, H * Dh)


def pool_mixer(u, w_pool, pool_scale):
    B, S, _ = u.shape
    uf = u.astype(jnp.float32)
    c = jnp.pad(jnp.cumsum(uf, axis=1), ((0, 0), (1, 0), (0, 0)))
    outs = []
    for gi, w in enumerate(POOL_WINDOWS):
        sl = slice(gi * POOL_GROUP_DIM, (gi + 1) * POOL_GROUP_DIM)
        cg = c[..., sl]
        upper = cg[:, 1:]
        lower = jnp.pad(cg, ((0, 0), (w - 1, 0), (0, 0)))[:, :S]
        cnt = jnp.minimum(jnp.arange(1, S + 1), w).astype(jnp.float32)[None, :, None]
        outs.append((upper - lower) / cnt - uf[..., sl])
    d = jnp.stack(outs, axis=2)
    y = jnp.einsum('bsgc,gcd->bsgd', d, w_pool.astype(jnp.float32)).reshape(B, S, POOL_WIDTH)
    return (y * pool_scale.astype(jnp.float32)).astype(u.dtype)


def gla_chunked(q, k, v, gk):
    B, S, H, dk = q.shape
    dv = v.shape[-1]
    nc = S // GLA_CHUNK

    def to_chunks(a):
        return a.reshape(B, nc, GLA_CHUNK, H, a.shape[-1]).transpose(1, 0, 3, 2, 4)

    q, k, v, gk = to_chunks(q * (GLA_DK ** -0.5)), to_chunks(k), to_chunks(v), to_chunks(gk)
    b = jnp.cumsum(gk, axis=3)
    b_last = b[..., -1:, :]
    q_t = q * jnp.exp(b)
    k_t = k * jnp.exp(-b)
    k_s = k * jnp.exp(b_last - b)
    causal = jnp.tril(jnp.ones((GLA_CHUNK, GLA_CHUNK), dtype=bool))
    a = jnp.where(causal, jnp.einsum('nbhid,nbhjd->nbhij', q_t, k_t), 0.0)
    o_intra = jnp.einsum('nbhij,nbhjv->nbhiv', a, v)

    def step(state, inp):
        qn, kn, vn, decay = inp
        o = jnp.einsum('bhid,bhdv->bhiv', qn, state)
        state = state * decay[:, :, 0, :, None] + jnp.einsum('bhjd,bhjv->bhdv', kn, vn)
        return state, o

    s0 = jnp.zeros((B, H, dk, dv), jnp.float32)
    _, o_inter = lax.scan(step, s0, (q_t, k_s, v, jnp.exp(b_last)))
    o = o_intra + o_inter
    return o.transpose(1, 0, 3, 2, 4).reshape(B, S, H, dv)


def hybrid_mixer(h, positions, w_in, b_gate, att_sinks, w_pool, pool_scale, w_gla_a2, b_gla_a,
                 gla_norm, w_br_att, w_br_pool, w_br_gla, w_out):
    B, S, _ = h.shape
    proj = h @ w_in
    sizes = [ATT_Q_W, ATT_KV_W, ATT_KV_W, POOL_WIDTH, GLA_QK_W, GLA_QK_W, GLA_V_W, GLA_V_W, GLA_LOWRANK]
    points, acc = [], 0
    for sz in sizes:
        acc += sz
        points.append(acc)
    qa, ka, va, pu, qg, kg, vg, og, lr, gate_logits = jnp.split(proj, points, axis=-1)

    qa = rope(qa.reshape(B, S, ATT_HEADS, HEAD_DIM), positions)
    ka = rope(ka.reshape(B, S, ATT_KV_HEADS, HEAD_DIM), positions)
    va = va.reshape(B, S, ATT_KV_HEADS, HEAD_DIM)
    y_att = sliding_window_sink_attention(qa, ka, va, att_sinks)

    y_pool = pool_mixer(pu, w_pool, pool_scale)

    f32 = jnp.float32
    gk = jax.nn.log_sigmoid((lr @ w_gla_a2).astype(f32) + b_gla_a.astype(f32)) / GLA_TAU
    o = gla_chunked(qg.astype(f32).reshape(B, S, GLA_HEADS, GLA_DK),
                    kg.astype(f32).reshape(B, S, GLA_HEADS, GLA_DK),
                    vg.astype(f32).reshape(B, S, GLA_HEADS, GLA_DV),
                    gk.reshape(B, S, GLA_HEADS, GLA_DK))
    o = o * lax.rsqrt(jnp.mean(o * o, axis=-1, keepdims=True) + EPS)
    o = o.reshape(B, S, GLA_V_W) * gla_norm.astype(f32) * jax.nn.silu(og.astype(f32))
    y_gla = o.astype(h.dtype)

    gates = jax.nn.sigmoid(gate_logits.astype(f32) + b_gate.astype(f32)).reshape(B, S, N_BRANCH, D_MODEL)
    merged = (gates[:, :, 0] * (y_att @ w_br_att).astype(f32)
              + gates[:, :, 1] * (y_pool @ w_br_pool).astype(f32)
              + gates[:, :, 2] * (y_gla @ w_br_gla).astype(f32))
    return merged.astype(h.dtype) @ w_out


def setup_inputs(seed: int = 0) -> dict:
    key = jax.random.key(seed)
    ks = jax.random.split(key, 24)
    L, D, F = DEPTH, D_MODEL, D_FF

    def nrm(k, shape, scale):
        return jax.random.normal(k, shape, jnp.float32) * scale

    def gain(k, shape):
        return 1.0 + 0.02 * jax.random.normal(k, shape, jnp.float32)

    offsets = jax.random.randint(ks[1], (BATCH, 1), 0, 4096, dtype=jnp.int32)
    positions = offsets + jnp.arange(SEQ, dtype=jnp.int32)[None, :]
    return {
        "x": jax.random.normal(ks[0], (BATCH, SEQ, D), jnp.float32),
        "positions": positions,
        "norm_ffn1": gain(ks[2], (L, D)),
        "ffn1_wi": nrm(ks[3], (L, D, 2 * F), D ** -0.5),
        "ffn1_wo": nrm(ks[4], (L, F, D), F ** -0.5),
        "norm_mix": gain(ks[5], (L, D)),
        "w_in": nrm(ks[6], (L, D, W_IN_COLS), D ** -0.5),
        "b_gate": nrm(ks[7], (L, N_BRANCH * D), 0.02),
        "att_sinks": nrm(ks[8], (L, ATT_HEADS), 0.5),
        "w_pool": nrm(ks[9], (L, POOL_GROUPS, POOL_GROUP_DIM, POOL_GROUP_DIM), POOL_GROUP_DIM ** -0.5),
        "pool_scale": gain(ks[10], (L, POOL_WIDTH)),
        "w_gla_a2": nrm(ks[11], (L, GLA_LOWRANK, GLA_QK_W), GLA_LOWRANK ** -0.5),
        "b_gla_a": nrm(ks[12], (L, GLA_QK_W), 0.1),
        "gla_norm": gain(ks[13], (L, GLA_V_W)),
        "w_br_att": nrm(ks[14], (L, ATT_Q_W, D), ATT_Q_W ** -0.5),
        "w_br_pool": nrm(ks[15], (L, POOL_WIDTH, D), POOL_WIDTH ** -0.5),
        "w_br_gla": nrm(ks[16], (L, GLA_V_W, D), GLA_V_W ** -0.5),
        "w_out": nrm(ks[17], (L, D, D), D ** -0.5),
        "norm_ffn2": gain(ks[18], (L, D)),
        "ffn2_wi": nrm(ks[19], (L, D, 2 * F), D ** -0.5),
        "ffn2_wo": nrm(ks[20], (L, F, D), F ** -0.5),
        "norm_final": gain(ks[21], (D,)),
    }


def reference(x, positions, norm_ffn1, ffn1_wi, ffn1_wo, norm_mix, w_in, b_gate, att_sinks, w_pool,
              pool_scale, w_gla_a2, b_gla_a, gla_norm, w_br_att, w_br_pool, w_br_gla, w_out,
              norm_ffn2, ffn2_wi, ffn2_wo, norm_final):
    for l in range(DEPTH):
        x = x + 0.5 * swiglu(rms_norm(x, norm_ffn1[l]), ffn1_wi[l], ffn1_wo[l])
        h = rms_norm(x, norm_mix[l])
        x = x + hybrid_mixer(h, positions, w_in[l], b_gate[l], att_sinks[l], w_pool[l], pool_scale[l],
                             w_gla_a2[l], b_gla_a[l], gla_norm[l], w_br_att[l], w_br_pool[l],
                             w_br_gla[l], w_out[l])
        x = x + 0.5 * swiglu(rms_norm(x, norm_ffn2[l]), ffn2_wi[l], ffn2_wo[l])
    return rms_norm(x, norm_final)
```

```python
import numpy as np
import ml_dtypes
import concourse.bass as bass
import concourse.mybir as mybir
from concourse.bass_utils import run_bass_kernel_spmd

F32 = mybir.dt.float32
BF16 = mybir.dt.bfloat16
I32 = mybir.dt.int32
U8 = mybir.dt.uint8
AF = mybir.ActivationFunctionType
ALU = mybir.AluOpType
AX = mybir.AxisListType
DTSZ = {F32: 4, BF16: 2, I32: 4, U8: 1}

D = 2048
NT = 8
NTOK = NT * 128
L_ALL = 4
DFF = 5632
NCH = DFF // 128
WIN = 10512
C_Q, C_K, C_V, C_PU = 0, 768, 1024, 1280
C_GQ, C_GK, C_GV, C_GO, C_LR, C_GATE = 2048, 2432, 2816, 3584, 4352, 4368
EPS = 1e-6
GRAN = 128
NEG = -30000.0


class Ins:
    __slots__ = ("eng", "fn", "deps", "sig", "val", "is_dma", "dsem", "dval", "dprev")

    def __init__(self, eng, fn, is_dma):
        self.eng = eng
        self.fn = fn
        self.deps = set()
        self.sig = False
        self.val = 0
        self.is_dma = is_dma
        self.dsem = None
        self.dval = 0
        self.dprev = None


class Sched:
    ENGS = ("pe", "act", "dve", "pool", "sp")

    def __init__(self, nc, ndma_sems=12):
        self.nc = nc
        self.lists = {e: [] for e in self.ENGS}
        self.track = {}
        self.gcache = {}
        self.eng_sem = {e: nc.alloc_semaphore("sem_" + e) for e in ("pe", "act", "dve", "pool")}
        self.dma_sems = {q: [nc.alloc_semaphore("dsem_%s_%d" % (q, i)) for i in range(ndma_sems)]
                         for q in ("sp", "pool")}
        self.dma_hist = {q: [] for q in ("sp", "pool")}
        self.stores = []

    def granules(self, ap):
        key = (ap.tensor.name, ap.offset, tuple(ap.ap), str(ap.dtype))
        g = self.gcache.get(key)
        if g is not None:
            return g
        sz = DTSZ[ap.dtype]
        dims = list(ap.ap)
        row = dims[0][0]
        off = ap.offset % row if row > 0 else ap.offset
        free = dims[1:]
        name = ap.tensor.name
        res = set()
        if not free:
            free = [(1, 1)]
        last = free[-1]
        outer = free[:-1]
        bases = [off]
        for (st, cnt) in outer:
            bases = [b + st * i for b in bases for i in range(cnt)]
        ls, lc = last
        span = 1 if ls == 0 else ls * (lc - 1) + 1
        for b in bases:
            lo = (b * sz) // GRAN
            hi = ((b + span) * sz - 1) // GRAN
            for gi in range(lo, hi + 1):
                res.add((name, gi))
        g = frozenset(res)
        self.gcache[key] = g
        return g

    def op(self, eng, fn, reads=(), writes=(), is_dma=False, store=False):
        ins = Ins(eng, fn, is_dma)
        deps = ins.deps
        tr = self.track
        for ap in reads:
            for g in self.granules(ap):
                e = tr.get(g)
                if e is None:
                    tr[g] = [None, [ins]]
                else:
                    if e[0] is not None:
                        deps.add(e[0])
                    e[1].append(ins)
        for ap in writes:
            for g in self.granules(ap):
                e = tr.get(g)
                if e is None:
                    tr[g] = [ins, []]
                else:
                    if e[0] is not None:
                        deps.add(e[0])
                    for r in e[1]:
                        deps.add(r)
                    e[0] = ins
                    e[1] = []
        deps.discard(ins)
        if eng == "pe":
            ins.deps = deps = {d for d in deps if not (d.eng == "pe" and not d.is_dma)}
        for d in deps:
            d.sig = True
        if is_dma:
            h = self.dma_hist[eng]
            j = len(h)
            sems = self.dma_sems[eng]
            k = len(sems)
            ins.dsem = sems[j % k]
            ins.dval = 16 * (j // k + 1)
            if j >= k:
                ins.dprev = h[j - k]
            h.append(ins)
            if store:
                self.stores.append(ins)
        self.lists[eng].append(ins)
        return ins

    def emit(self):
        nc = self.nc
        for e in self.ENGS:
            cnt = 0
            for ins in self.lists[e]:
                if not ins.is_dma and ins.sig:
                    cnt += 1
                    ins.val = cnt
        handles = {"pe": nc.tensor, "act": nc.scalar, "dve": nc.vector, "pool": nc.gpsimd, "sp": nc.sync}
        bnames = {"pe": "tensor", "act": "scalar", "dve": "vector", "pool": "gpsimd", "sp": "sync"}
        nwaits = {e: 0 for e in self.ENGS}

        def run_engine(e):
            h = handles[e]
            waited = {}

            def wait(sem, val):
                key = id(sem)
                if waited.get(key, 0) >= val:
                    return
                h.wait_ge(sem, val)
                waited[key] = val
                nwaits[e] += 1

            for ins in self.lists[e]:
                for d in ins.deps:
                    if d.is_dma:
                        wait(d.dsem, d.dval)
                    else:
                        wait(self.eng_sem[d.eng], d.val)
                if ins.is_dma and ins.dprev is not None:
                    wait(ins.dprev.dsem, ins.dprev.dval)
                inst = ins.fn()
                if ins.is_dma:
                    inst.then_inc(ins.dsem, 16)
                elif ins.sig:
                    inst.then_inc(self.eng_sem[e], 1)
            if e == "sp":
                for st in self.stores:
                    wait(st.dsem, st.dval)

        with nc.Block() as block:
            for e in self.ENGS:
                getattr(block, bnames[e])(lambda _h, _e=e: run_engine(_e))
        self.nwaits = nwaits


class Prog:
    def __init__(self, n_layers=L_ALL, final_norm=True):
        self.n_layers = n_layers
        self.final_norm = final_norm
        nc = self.nc = bass.Bass("TRN2", target_bir_lowering=False)
        self.S = Sched(nc)
        L = L_ALL
        dt = nc.dram_tensor

        def din(name, shape, dtype):
            return dt(name, list(shape), dtype, kind="ExternalInput").ap()

        def dout(name, shape, dtype):
            return dt(name, list(shape), dtype, kind="ExternalOutput").ap()

        self.d = {}
        for name, shape, dtype in [
            ("x", (NTOK, D), F32), ("pos", (128, NT), I32), ("invf", (128, 32), F32),
            ("norm_ffn1_t", (L, 128, 16), F32), ("norm_mix_t", (L, 128, 16), F32), ("norm_ffn2_t", (L, 128, 16), F32),
            ("norm_final", (1, D), F32),
            ("ffn1_wi", (L, D, 2 * DFF), F32), ("ffn1_wo", (L, DFF, D), F32),
            ("ffn2_wi", (L, D, 2 * DFF), F32), ("ffn2_wo", (L, DFF, D), F32),
            ("w_in", (L, D, WIN), F32), ("b_gate_t", (L, 128, 48), F32), ("att_sinks", (L, 12), F32),
            ("w_pool", (L, 4, 192, 192), F32), ("pool_scale_t", (L, 96, 8), F32),
            ("w_gla_a2", (L, 16, 384), F32), ("b_gla_a", (L, 384), F32), ("gla_norm", (L, 768), F32),
            ("w_br_att", (L, 768, D), F32), ("w_br_pool", (L, 768, D), F32), ("w_br_gla", (L, 768, D), F32),
            ("w_out", (L, D, D), F32),
            ("ident", (128, 128), BF16), ("maskA", (128, 256), F32), ("mask0", (128, 256), F32),
            ("tri_incl", (128, 128), BF16), ("tri_su", (128, 128), BF16), ("maskT", (128, 128), BF16),
            ("bands", (128, 8, 128), BF16), ("bands0", (128, 8, 128), BF16),
            ("c_kt", (L, 64, 512), BF16), ("c_v", (L, 128, 256), BF16), ("c_u", (L, 128, 768), BF16),
            ("c_s", (L, 96, 768), F32),
        ]:
            self.d[name] = din(name, shape, dtype)
        self.o_y = dout("y", (NTOK, D), F32)
        self.o_kt = dout("o_kt", (L, 64, 512), BF16)
        self.o_v = dout("o_v", (L, 128, 256), BF16)
        self.o_u = dout("o_u", (L, 128, 768), BF16)
        self.o_s = dout("o_s", (L, 96, 768), F32)

        self.arena_bytes = 212000
        self.A = nc.alloc_sbuf_tensor("arena", [128, self.arena_bytes], U8)
        self.top = 0
        al = self.alloc
        self.x = al(F32, NT, D)
        self.hT = al(BF16, 16, NTOK)
        self.ring = [al(BF16, 4096) for _ in range(4)]
        self.ring_i = 0
        self.ident = al(BF16, 128)
        self.maskA = al(F32, 256)
        self.mask0 = al(F32, 256)
        self.tri_incl = al(BF16, 128)
        self.tri_su = al(BF16, 128)
        self.maskT = al(BF16, 128)
        self.bands = al(BF16, 8, 128)
        self.bands0 = al(BF16, 8, 128)
        self.cos = al(F32, NT, 32)
        self.sin = al(F32, NT, 32)
        self.cst = al(F32, 8)
        self.stat = al(F32, 32)
        self.gains = al(F32, 3, 16)
        self.bgate = al(F32, 48)
        self.sinks = al(F32, 12)
        self.pscale = al(F32, 8, parts=96)
        self.ba_bc = al(F32, 384)
        self.gn_bc = al(F32, 768)
        self.w_a2 = al(BF16, 384, parts=16)
        self.w_pool = al(BF16, 8, 192, parts=96)
        self.KT = al(BF16, 4, 5, 128, parts=64)
        self.Vb = al(BF16, 5, 256)
        self.uh = al(BF16, 768)
        self.Sst = al(F32, 4, 192, parts=96)
        self.Sbf = al(BF16, 4, 192, parts=96)
        big_off = self.alloc_bytes(4 * 2320 * 2)
        self.p_tm = self.view(big_off, 128, BF16, 4, 2320)
        self.gT = self.view(big_off, 128, BF16, 8, NTOK)
        yt_off = self.alloc_bytes(8192)
        self.yT128 = self.view(yt_off, 128, BF16, 6, 512)
        self.yT96 = self.view(yt_off, 96, BF16, 8, 512)
        self.tmp_off = self.top
        self.tmp_top = self.top
        assert self.top <= self.arena_bytes, self.top
        self.banks = [nc.alloc_psum_tensor("ps%d" % i, [128, 512], F32) for i in range(8)]
        self.bank_i = 0

    def alloc_bytes(self, n, align=64):
        off = (self.top + align - 1) // align * align
        self.top = off + n
        assert self.top <= self.arena_bytes, ("SBUF overflow", self.top)
        return off

    def view(self, off, parts, dtype, *dims):
        n = int(np.prod(dims)) * DTSZ[dtype]
        ap = self.A[0:parts, off:off + n].bitcast(dtype)
        if len(dims) == 2:
            ap = ap.rearrange("p (a b) -> p a b", b=dims[1])
        elif len(dims) == 3:
            ap = ap.rearrange("p (a b c) -> p a b c", b=dims[1], c=dims[2])
        return ap

    def alloc(self, dtype, *dims, parts=128):
        off = self.alloc_bytes(int(np.prod(dims)) * DTSZ[dtype])
        return self.view(off, parts, dtype, *dims)

    def tmp_reset(self):
        self.tmp_top = self.tmp_off

    def tmp(self, dtype, *dims, parts=128):
        n = int(np.prod(dims)) * DTSZ[dtype]
        off = (self.tmp_top + 63) // 64 * 64
        self.tmp_top = off + n
        assert self.tmp_top <= self.arena_bytes, ("SBUF tmp overflow", self.tmp_top)
        return self.view(off, parts, dtype, *dims)

    def bank(self):
        b = self.banks[self.bank_i % 8]
        self.bank_i += 1
        return b

    def slot(self):
        s = self.ring[self.ring_i % 4]
        self.ring_i += 1
        return s

    def mm(self, out, lhsT, rhs, start=True, stop=True):
        nc = self.nc
        self.S.op("pe", lambda: nc.tensor.matmul(out, lhsT=lhsT, rhs=rhs, start=start, stop=stop),
                  reads=[lhsT, rhs], writes=[out])

    def tr(self, out, in_):
        nc = self.nc
        ident = self.ident
        self.S.op("pe", lambda: nc.tensor.transpose(out, in_, ident), reads=[in_, ident], writes=[out])

    def act(self, out, in_, func, bias=None, scale=None, accum=None):
        nc = self.nc
        kw = {}
        reads = [in_]
        if bias is not None:
            kw["bias"] = bias
            if not isinstance(bias, (int, float)):
                reads.append(bias)
        if scale is not None:
            kw["scale"] = scale
            if not isinstance(scale, (int, float)):
                reads.append(scale)
        writes = [out]
        if accum is not None:
            kw["accum_out"] = accum
            writes.append(accum)
        self.S.op("act", lambda: nc.scalar.activation(out=out, in_=in_, func=func, **kw), reads=reads, writes=writes)

    def tt(self, out, in0, in1, op, eng="dve"):
        nc = self.nc
        e = nc.vector if eng == "dve" else nc.gpsimd
        self.S.op(eng, lambda: e.tensor_tensor(out=out, in0=in0, in1=in1, op=op), reads=[in0, in1], writes=[out])

    def ts(self, out, in0, s1, op0, s2=None, op1=None, eng="dve"):
        nc = self.nc
        e = nc.vector if eng == "dve" else nc.gpsimd
        reads = [in0]
        if not isinstance(s1, (int, float)):
            reads.append(s1)
        if s2 is not None and not isinstance(s2, (int, float)):
            reads.append(s2)
        if op1 is None:
            f = lambda: e.tensor_scalar(out=out, in0=in0, scalar1=s1, scalar2=None, op0=op0)
        else:
            f = lambda: e.tensor_scalar(out=out, in0=in0, scalar1=s1, scalar2=s2, op0=op0, op1=op1)
        self.S.op(eng, f, reads=reads, writes=[out])

    def stt(self, out, in0, scalar, in1, op0, op1):
        nc = self.nc
        reads = [in0, in1]
        if not isinstance(scalar, (int, float)):
            reads.append(scalar)
        self.S.op("dve", lambda: nc.vector.scalar_tensor_tensor(out=out, in0=in0, scalar=scalar, in1=in1,
                                                                 op0=op0, op1=op1), reads=reads, writes=[out])

    def red(self, out, in_, op, axis=AX.X):
        nc = self.nc
        self.S.op("dve", lambda: nc.vector.tensor_reduce(out=out, in_=in_, axis=axis, op=op), reads=[in_], writes=[out])

    def cp(self, out, in_, eng="dve"):
        nc = self.nc
        if eng == "act":
            self.S.op("act", lambda: nc.scalar.copy(out=out, in_=in_), reads=[in_], writes=[out])
        else:
            e = nc.vector if eng == "dve" else nc.gpsimd
            self.S.op(eng, lambda: e.tensor_copy(out=out, in_=in_), reads=[in_], writes=[out])

    def recip(self, out, in_):
        nc = self.nc
        self.S.op("dve", lambda: nc.vector.reciprocal(out=out, in_=in_), reads=[in_], writes=[out])

    def memset(self, out, val):
        nc = self.nc
        self.S.op("dve", lambda: nc.vector.memset(out, val), reads=[], writes=[out])

    def load(self, out, in_, cast=False):
        nc = self.nc
        if cast:
            self.S.op("pool", lambda: nc.gpsimd.dma_start(out=out, in_=in_), reads=[], writes=[out], is_dma=True)
        else:
            self.S.op("sp", lambda: nc.sync.dma_start(out=out, in_=in_), reads=[], writes=[out], is_dma=True)

    @staticmethod
    def pbc(row):
        return row.partition_broadcast(128)[:, 0, :]

    def store(self, out, in_):
        nc = self.nc
        self.S.op("sp", lambda: nc.sync.dma_start(out=out, in_=in_), reads=[in_], writes=[], is_dma=True, store=True)

    def wblock(self, w2d, c0, ncols, kparts=128, r0=0, nk=16):
        s = self.slot()
        v = s[0:kparts, 0:nk * ncols].rearrange("p (k n) -> p k n", n=ncols)
        src = w2d[r0:r0 + nk * kparts, c0:c0 + ncols].rearrange("(k p) n -> p k n", p=kparts)
        self.load(v, src, cast=True)
        return v

    def setup(self):
        d = self.d
        xv = d["x"].rearrange("(t p) d -> p t d", p=128)
        for t in range(NT):
            self.load(self.x[:, t, :], xv[:, t, :])
        for name, dst in [("ident", self.ident), ("maskA", self.maskA), ("mask0", self.mask0),
                          ("tri_incl", self.tri_incl), ("tri_su", self.tri_su), ("maskT", self.maskT),
                          ("bands", self.bands), ("bands0", self.bands0)]:
            self.load(dst, d[name])
        self.memset(self.cst[:, 0:1], EPS)
        self.memset(self.cst[:, 1:2], 1.0)
        self.memset(self.cst[:, 2:3], 0.0)
        self.tmp_reset()
        posi = self.tmp(I32, NT)
        invf = self.tmp(F32, 32)
        posf = self.tmp(F32, NT)
        ang = self.tmp(F32, NT, 32)
        a2 = self.tmp(F32, NT, 32)
        kq = self.tmp(F32, NT, 32)
        ki = self.tmp(I32, NT, 32)
        kf = self.tmp(F32, NT, 32)
        m = self.tmp(F32, NT, 32)
        self.load(posi, d["pos"])
        self.load(invf, d["invf"])
        self.cp(posf, posi)
        self.tt(ang, posf.unsqueeze(2).broadcast_to([128, NT, 32]), invf.unsqueeze(1).broadcast_to([128, NT, 32]), ALU.mult)
        TWO_PI = 2.0 * np.pi
        C1 = 6.28125
        C2 = TWO_PI - C1
        for shift, dst in [(0.0, self.sin), (np.pi / 2, self.cos)]:
            if shift != 0.0:
                self.ts(a2, ang, float(shift), ALU.add)
                a = a2
            else:
                a = ang
            self.ts(kq, a, float(1.0 / TWO_PI), ALU.mult)
            self.cp(ki, kq)
            self.cp(kf, ki)
            self.stt(m, kf, float(-C1), a, ALU.mult, ALU.add)
            self.stt(m, kf, float(-C2), m, ALU.mult, ALU.add)
            self.ts(kq, m, float(np.pi), ALU.is_gt, float(-TWO_PI), ALU.mult)
            self.tt(m, m, kq, ALU.add)
            self.ts(kq, m, float(-np.pi), ALU.is_lt, float(TWO_PI), ALU.mult)
            self.tt(m, m, kq, ALU.add)
            self.ts(m, m, 3.1415925, ALU.min, -3.1415925, ALU.max)
            self.act(dst, m, AF.Sin)

    def layer_params(self, l):
        d = self.d
        self.load(self.gains[:, 0, :], d["norm_ffn1_t"][l])
        self.load(self.gains[:, 1, :], d["norm_mix_t"][l])
        self.load(self.gains[:, 2, :], d["norm_ffn2_t"][l])
        self.load(self.bgate, d["b_gate_t"][l])
        self.load(self.sinks, self.pbc(d["att_sinks"][l:l + 1, :]))
        self.load(self.pscale, d["pool_scale_t"][l])
        self.load(self.ba_bc, self.pbc(d["b_gla_a"][l:l + 1, :]))
        self.load(self.gn_bc, self.pbc(d["gla_norm"][l:l + 1, :]))
        self.load(self.w_a2, d["w_gla_a2"][l], cast=True)
        self.load(self.w_pool, d["w_pool"][l].rearrange("g (i p) d -> p (g i) d", p=96), cast=True)

    def rstd_of(self, xt, st):
        junk = self.tmp(BF16, D)
        self.act(junk, xt, AF.Square, accum=st[:, 0:1])
        self.act(st[:, 1:2], st[:, 0:1], AF.Ln, scale=1.0 / D, bias=self.cst[:, 0:1])
        self.act(st[:, 2:3], st[:, 1:2], AF.Exp, scale=-0.5)

    def norm_T(self, tiles, gi, col0):
        self.tmp_reset()
        sts = [self.tmp(F32, 4) for _ in range(2)]
        xss = [self.tmp(BF16, D) for _ in range(2)]
        gain = self.gains[:, gi, :]
        for j, t in enumerate(tiles):
            st = sts[j % 2]
            xs = xss[j % 2]
            save = self.tmp_top
            self.rstd_of(self.x[:, t, :], st)
            self.tmp_top = save
            self.ts(xs, self.x[:, t, :], st[:, 2:3], ALU.mult)
            for hb in range(2):
                pb = self.bank()[:, 0:512].bitcast(BF16).rearrange("p (k n) -> p k n", n=128)
                for k in range(8):
                    kc = hb * 8 + k
                    self.tr(pb[:, k, :], xs[:, kc * 128:(kc + 1) * 128])
                c = col0 + j * 128
                self.tt(self.hT[:, hb * 8:(hb + 1) * 8, c:c + 128], pb,
                        gain[:, hb * 8:(hb + 1) * 8].unsqueeze(2).broadcast_to([128, 8, 128]), ALU.mult)

    def ffn(self, wi, wo):
        self.tmp_reset()
        sil = [self.tmp(F32, 512) for _ in range(2)]
        si = 0
        rounds = [(c0, min(8, NCH - c0)) for c0 in range(0, NCH, 8)]
        for (c0, nchk) in rounds:
            for cp2 in range(0, nchk, 2):
                ch0 = c0 + cp2
                wa = self.wblock(wi, ch0 * 128, 256)
                wb = self.wblock(wi, DFF + ch0 * 128, 256)
                for cc in range(2):
                    lc = cp2 + cc
                    for th in range(NTOK // 512):
                        pa = self.bank()
                        pb = self.bank()
                        for kc in range(16):
                            self.mm(pa[:, :], wa[:, kc, cc * 128:(cc + 1) * 128], self.hT[:, kc, th * 512:(th + 1) * 512],
                                    start=(kc == 0), stop=(kc == 15))
                        for kc in range(16):
                            self.mm(pb[:, :], wb[:, kc, cc * 128:(cc + 1) * 128], self.hT[:, kc, th * 512:(th + 1) * 512],
                                    start=(kc == 0), stop=(kc == 15))
                        s = sil[si % 2]
                        si += 1
                        self.act(s, pa[:, :], AF.Silu)
                        self.tt(self.gT[:, lc, th * 512:(th + 1) * 512], s, pb[:, :], ALU.mult)
            for dg in range(4):
                wv = self.wblock(wo, dg * 512, 512, r0=c0 * 128, nk=nchk)
                for t in range(NT):
                    po = self.bank()
                    for lc in range(nchk):
                        self.mm(po[:, :], self.gT[:, lc, t * 128:(t + 1) * 128], wv[:, lc, :],
                                start=(lc == 0), stop=(lc == nchk - 1))
                    xd = self.x[:, t, dg * 512:(dg + 1) * 512]
                    self.stt(xd, po[:, :], 0.5, xd, ALU.mult, ALU.add)

    def project(self, w_in, hf, c_start, c_end, base_col, evac):
        c = c_start
        while c < c_end:
            n = min(256, c_end - c)
            wv = self.wblock(w_in, c, n)
            for j in range(4):
                ps = self.bank()
                for kc in range(16):
                    self.mm(ps[:, 0:n], self.hT[:, kc, j * 128:(j + 1) * 128], wv[:, kc, :],
                            start=(kc == 0), stop=(kc == 15))
                evac(j, ps[:, 0:n], c - base_col, c - base_col + n)
            c += n

    def merge_branch(self, l, bi, w_in, w_br, yT, nyc, first):
        self.tmp_reset()
        mT = self.hT[:, :, 512:1024]
        sig = [self.tmp(F32, 512) for _ in range(2)]
        tmpz = [self.tmp(F32, 512) for _ in range(2)]
        kp = yT.shape[0]
        i = 0
        for ng in range(4):
            wgs = [self.wblock(w_in, C_GATE + bi * D + ng * 512 + q * 256, 256) for q in range(2)]
            wbr = self.wblock(w_br, ng * 512, 512, kparts=kp, nk=nyc)
            for nn in range(4):
                n = ng * 4 + nn
                wg = wgs[nn // 2]
                cc = (nn % 2) * 128
                pg = self.bank()
                pz = self.bank()
                for kc in range(16):
                    self.mm(pg[:, :], wg[:, kc, cc:cc + 128], self.hT[:, kc, 0:512], start=(kc == 0), stop=(kc == 15))
                for c in range(nyc):
                    self.mm(pz[:, :], wbr[:, c, nn * 128:(nn + 1) * 128], yT[:, c, :], start=(c == 0), stop=(c == nyc - 1))
                sg = sig[i % 2]
                tz = tmpz[i % 2]
                i += 1
                self.act(sg, pg[:, :], AF.Sigmoid, bias=self.bgate[:, bi * 16 + n:bi * 16 + n + 1])
                if first:
                    self.tt(mT[:, n, :], sg, pz[:, :], ALU.mult)
                else:
                    self.tt(tz, sg, pz[:, :], ALU.mult)
                    self.tt(mT[:, n, :], tz, mT[:, n, :], ALU.add)

    def attention(self, l, hf, w_in):
        self.tmp_reset()
        p_tm = self.p_tm
        cosv, sinv = self.cos, self.sin

        def evac(j, ps, lo, hi):
            t = hf * 4 + j
            if lo >= C_V:
                self.cp(self.Vb[:, j + 1, lo - C_V:hi - C_V], ps, eng="act")
                return
            nh = (hi - lo) // 64
            src = ps.rearrange("p (h two f) -> p h two f", two=2, f=32)
            dst = p_tm[:, j, lo:hi].rearrange("p (h two f) -> p h two f", two=2, f=32)
            cb = cosv[:, t, :].unsqueeze(1).broadcast_to([128, nh, 32])
            sb = sinv[:, t, :].unsqueeze(1).broadcast_to([128, nh, 32])
            t1 = self.rt1[:, 0:nh, :]
            t2 = self.rt2[:, 0:nh, :]
            x1 = src[:, :, 0, :]
            x2 = src[:, :, 1, :]
            self.tt(t1, x1, cb, ALU.mult)
            self.tt(t2, x2, sb, ALU.mult)
            self.tt(dst[:, :, 0, :], t1, t2, ALU.subtract)
            self.tt(t1, x2, cb, ALU.mult)
            self.tt(t2, x1, sb, ALU.mult)
            self.tt(dst[:, :, 1, :], t1, t2, ALU.add)

        self.rt1 = self.tmp(F32, 4, 32)
        self.rt2 = self.tmp(F32, 4, 32)
        if hf == 0:
            self.load(self.KT[:, :, 0, :], self.d["c_kt"][l].rearrange("p (h n) -> p h n", n=128))
            self.load(self.Vb[:, 0, :], self.d["c_v"][l])
        else:
            self.cp(self.KT[:, :, 0, :], self.KT[:, :, 4, :], eng="act")
            self.cp(self.Vb[:, 0, :], self.Vb[:, 4, :], eng="act")
        self.project(w_in, hf, 0, C_PU, 0, evac)
        QT = [self.tmp(BF16, 12, 128, parts=64) for _ in range(2)]
        sm = [self.tmp(F32, 2, 256) for _ in range(2)]
        ee = [self.tmp(BF16, 2, 256) for _ in range(2)]
        PT = [self.tmp(BF16, 4, 128) for _ in range(2)]
        yat = [self.tmp(BF16, 768) for _ in range(2)]
        stt_ = [self.tmp(F32, 16) for _ in range(2)]
        pi = 0
        for j in range(4):
            t = hf * 4 + j
            qt = QT[j % 2]
            ya = yat[j % 2]
            pk = self.bank()[0:64, 0:256].bitcast(BF16).rearrange("p (h n) -> p h n", n=128)
            for kh in range(4):
                self.tr(pk[:, kh, :], p_tm[:, j, C_K + kh * 64:C_K + (kh + 1) * 64])
            self.cp(self.KT[:, :, j + 1, :], pk, eng="act")
            for qb in range(2):
                pq = self.bank()[0:64, 0:384].bitcast(BF16).rearrange("p (h n) -> p h n", n=128)
                for hh in range(6):
                    h = qb * 6 + hh
                    self.tr(pq[:, hh, :], p_tm[:, j, h * 64:(h + 1) * 64])
                self.cp(qt[:, qb * 6:(qb + 1) * 6, :], pq, eng="act")
            mask = self.mask0 if (hf == 0 and j == 0) else self.maskA
            for hp in range(6):
                s_m = sm[pi % 2]
                e_ = ee[pi % 2]
                pt = PT[pi % 2]
                st = stt_[pi % 2]
                pi += 1
                ps = self.bank()
                psv = ps[:, :].rearrange("p (a n) -> p a n", n=256)
                for a in range(2):
                    h = hp * 2 + a
                    kh = h // 3
                    self.mm(psv[:, a, :], qt[:, h, :], self.KT[:, kh, j:j + 2, :].rearrange("p a n -> p (a n)"), start=True, stop=True)
                self.stt(s_m, psv, 0.125, mask.unsqueeze(1).broadcast_to([128, 2, 256]), ALU.mult, ALU.add)
                self.red(st[:, 0:2], s_m, ALU.max)
                self.tt(st[:, 0:2], st[:, 0:2], self.sinks[:, hp * 2:hp * 2 + 2], ALU.max)
                self.ts(st[:, 2:4], st[:, 0:2], -1.0, ALU.mult)
                for a in range(2):
                    self.act(e_[:, a, :], s_m[:, a, :], AF.Exp, bias=st[:, 2 + a:3 + a], accum=st[:, 4 + a:5 + a])
                self.tt(st[:, 6:8], self.sinks[:, hp * 2:hp * 2 + 2], st[:, 2:4], ALU.add)
                self.act(st[:, 8:10], st[:, 6:8], AF.Exp)
                self.tt(st[:, 10:12], st[:, 4:6], st[:, 8:10], ALU.add)
                self.recip(st[:, 12:14], st[:, 10:12])
                pp = self.bank()[:, 0:256].bitcast(BF16).rearrange("p (a n) -> p a n", n=128)
                for a in range(2):
                    for kb in range(2):
                        self.tr(pp[:, a * 2 + kb, :], e_[:, a, kb * 128:(kb + 1) * 128])
                self.cp(pt, pp, eng="act")
                po = self.bank()
                for a in range(2):
                    h = hp * 2 + a
                    kh = h // 3
                    for kb in range(2):
                        self.mm(po[:, a * 64:(a + 1) * 64], pt[:, a * 2 + kb, :],
                                self.Vb[:, j + kb, kh * 64:(kh + 1) * 64], start=(kb == 0), stop=(kb == 1))
                for a in range(2):
                    h = hp * 2 + a
                    self.ts(ya[:, h * 64:(h + 1) * 64], po[:, a * 64:(a + 1) * 64], st[:, 12 + a:13 + a], ALU.mult)
            py = self.bank()[:, 0:384].bitcast(BF16).rearrange("p (c n) -> p c n", n=128)
            for c in range(6):
                self.tr(py[:, c, :], ya[:, c * 128:(c + 1) * 128])
            self.cp(self.yT128[:, :, j * 128:(j + 1) * 128], py, eng="act")
        if hf == 1:
            self.store(self.o_kt[l].rearrange("p (h n) -> p h n", n=128), self.KT[:, :, 4, :])
            self.store(self.o_v[l], self.Vb[:, 4, :])

    def pool(self, l, hf, w_in):
        self.tmp_reset()
        p_tm = self.p_tm

        def evac(j, ps, lo, hi):
            self.cp(p_tm[:, j, lo:hi], ps, eng="act")

        if hf == 0:
            self.load(self.uh, self.d["c_u"][l])
        self.project(w_in, hf, C_PU, C_GQ, C_PU, evac)
        dTs = [self.tmp(BF16, 8, 128, parts=96) for _ in range(2)]
        for j in range(4):
            dT = dTs[j % 2]
            first = (hf == 0 and j == 0)
            bands = self.bands0 if first else self.bands
            prev = self.uh if j == 0 else p_tm[:, j - 1, 0:768]
            cur = p_tm[:, j, 0:768]
            for half in range(2):
                pd = self.bank()[0:96, :].rearrange("p (c n) -> p c n", n=128)
                for cc in range(4):
                    c = half * 4 + cc
                    g = c // 2
                    self.mm(pd[:, cc, :], cur[:, c * 96:(c + 1) * 96], bands[:, g, :], start=True, stop=False)
                    self.mm(pd[:, cc, :], prev[:, c * 96:(c + 1) * 96], bands[:, 4 + g, :], start=False, stop=True)
                self.cp(dT[:, half * 4:(half + 1) * 4, :], pd, eng="act")
            for half in range(2):
                py = self.bank()[0:96, :].rearrange("p (c n) -> p c n", n=128)
                for cc in range(4):
                    c = half * 4 + cc
                    g, oc = c // 2, c % 2
                    for ic in range(2):
                        self.mm(py[:, cc, :], self.w_pool[:, g * 2 + ic, oc * 96:(oc + 1) * 96], dT[:, g * 2 + ic, :],
                                start=(ic == 0), stop=(ic == 1))
                self.tt(self.yT96[:, half * 4:(half + 1) * 4, j * 128:(j + 1) * 128], py,
                        self.pscale[:, half * 4:(half + 1) * 4].unsqueeze(2).broadcast_to([96, 4, 128]), ALU.mult)
        self.cp(self.uh, p_tm[:, 3, 0:768], eng="act")
        if hf == 1:
            self.store(self.o_u[l], self.uh)

    def gla(self, l, hf, w_in):
        self.tmp_reset()
        p_tm = self.p_tm
        G0 = C_GQ

        def evac(j, ps, lo, hi):
            og_lo, og_hi = C_GO - G0, C_LR - G0
            a, b = max(lo, og_lo), min(hi, og_hi)
            if a < b:
                self.act(p_tm[:, j, a:b], ps[:, a - lo:b - lo], AF.Silu)
            if lo < og_lo:
                b2 = min(hi, og_lo)
                self.cp(p_tm[:, j, lo:b2], ps[:, 0:b2 - lo], eng="act")
            if hi > og_hi:
                a2 = max(lo, og_hi)
                self.cp(p_tm[:, j, a2:hi], ps[:, a2 - lo:hi - lo], eng="act")

        if hf == 0:
            self.load(self.Sst, self.d["c_s"][l].rearrange("p (h v) -> p h v", v=192))
            self.cp(self.Sbf, self.Sst, eng="act")
        self.project(w_in, hf, C_GQ, C_GATE, C_GQ, evac)
        Q0, K0, V0, O0, R0 = 0, C_GK - G0, C_GV - G0, C_GO - G0, C_LR - G0
        lrT = self.tmp(BF16, 128, parts=16)
        zb = self.tmp(F32, 384)
        ax = self.tmp(F32, 384)
        mn = self.tmp(F32, 384)
        gk = self.tmp(BF16, 384)
        eb = self.tmp(F32, 384)
        enb = self.tmp(F32, 384)
        er = self.tmp(F32, 384)
        qt = self.tmp(BF16, 384)
        kt = self.tmp(BF16, 384)
        ks = self.tmp(BF16, 384)
        qkT = self.tmp(BF16, 8, 128, parts=96)
        ATm = self.tmp(BF16, 4, 128)
        dec = self.tmp(F32, 4, parts=96)
        st = self.tmp(F32, 16)
        otmp = self.tmp(F32, 768)
        yg = self.tmp(BF16, 768)
        onesc = self.tri_incl[:, 127:128]
        for j in range(4):
            pt = self.bank()[0:16, 0:64].bitcast(BF16)
            self.tr(pt, p_tm[:, j, R0:R0 + 16])
            self.cp(lrT, pt, eng="act")
            pz = self.bank()
            self.mm(pz[:, 0:384], lrT, self.w_a2, start=True, stop=True)
            self.tt(zb, pz[:, 0:384], self.ba_bc, ALU.add)
            self.stt(ax, zb, -1.0, zb, ALU.mult, ALU.max)
            self.act(ax, ax, AF.Exp, scale=-1.0)
            self.act(ax, ax, AF.Ln, bias=self.cst[:, 1:2])
            self.ts(mn, zb, 0.0, ALU.min)
            self.stt(mn, ax, -1.0, mn, ALU.mult, ALU.add)
            self.ts(gk, mn, 1.0 / 16.0, ALU.mult)
            pb_ = self.bank()
            pr_ = self.bank()
            self.mm(pb_[:, 0:384], self.tri_incl, gk, start=True, stop=True)
            self.mm(pr_[:, 0:384], self.tri_su, gk, start=True, stop=True)
            pd_ = self.bank()
            for h in range(4):
                self.mm(pd_[0:96, h:h + 1], gk[:, h * 96:(h + 1) * 96], onesc, start=True, stop=True)
            self.act(eb, pb_[:, 0:384], AF.Exp)
            self.act(enb, pb_[:, 0:384], AF.Exp, scale=-1.0)
            self.act(er, pr_[:, 0:384], AF.Exp)
            self.act(dec, pd_[0:96, 0:4], AF.Exp)
            self.stt(qt, p_tm[:, j, Q0:Q0 + 384], float(96 ** -0.5), eb, ALU.mult, ALU.mult)
            self.tt(kt, p_tm[:, j, K0:K0 + 384], enb, ALU.mult)
            self.tt(ks, p_tm[:, j, K0:K0 + 384], er, ALU.mult)
            pq = self.bank()[0:96, :].bitcast(BF16).rearrange("p (c n) -> p c n", n=128)
            for h in range(4):
                self.tr(pq[:, h, :], qt[:, h * 96:(h + 1) * 96])
                self.tr(pq[:, 4 + h, :], kt[:, h * 96:(h + 1) * 96])
            self.cp(qkT, pq[:, 0:8, :], eng="act")
            pa = self.bank()[:, :].rearrange("p (h n) -> p h n", n=128)
            for h in range(4):
                self.mm(pa[:, h, :], qkT[:, 4 + h, :], qkT[:, h, :], start=True, stop=True)
            self.tt(ATm, pa, self.maskT.unsqueeze(1).broadcast_to([128, 4, 128]), ALU.mult)
            po = [self.bank(), self.bank()]
            for h in range(4):
                o = po[h // 2][:, (h % 2) * 192:(h % 2 + 1) * 192]
                self.mm(o, ATm[:, h, :], p_tm[:, j, V0 + h * 192:V0 + (h + 1) * 192], start=True, stop=False)
                self.mm(o, qkT[:, h, :], self.Sbf[:, h, :], start=False, stop=True)
            pS = [self.bank(), self.bank()]
            for h in range(4):
                dS = pS[h // 2][0:96, (h % 2) * 192:(h % 2 + 1) * 192]
                self.mm(dS, ks[:, h * 96:(h + 1) * 96], p_tm[:, j, V0 + h * 192:V0 + (h + 1) * 192], start=True, stop=True)
            for h in range(4):
                dS = pS[h // 2][0:96, (h % 2) * 192:(h % 2 + 1) * 192]
                self.stt(self.Sst[:, h, :], self.Sst[:, h, :], dec[:, h:h + 1], dS, ALU.mult, ALU.add)
            self.cp(self.Sbf, self.Sst, eng="act")
            for h in range(4):
                o = po[h // 2][:, (h % 2) * 192:(h % 2 + 1) * 192]
                self.act(otmp[:, h * 192:(h + 1) * 192], o, AF.Square, accum=st[:, h:h + 1])
            self.act(st[:, 4:8], st[:, 0:4], AF.Ln, scale=1.0 / 192.0, bias=self.cst[:, 0:1])
            self.act(st[:, 8:12], st[:, 4:8], AF.Exp, scale=-0.5)
            for h in range(4):
                o = po[h // 2][:, (h % 2) * 192:(h % 2 + 1) * 192]
                self.stt(otmp[:, h * 192:(h + 1) * 192], o, st[:, 8 + h:9 + h], self.gn_bc[:, h * 192:(h + 1) * 192],
                         ALU.mult, ALU.mult)
            self.tt(yg, otmp, p_tm[:, j, O0:O0 + 768], ALU.mult)
            py = self.bank()[0:96, :].bitcast(BF16).rearrange("p (c n) -> p c n", n=128)
            for c in range(8):
                self.tr(py[:, c, :], yg[:, c * 96:(c + 1) * 96])
            self.cp(self.yT96[:, :, j * 128:(j + 1) * 128], py[:, 0:8, :], eng="act")
        if hf == 1:
            self.store(self.o_s[l].rearrange("p (h v) -> p h v", v=192), self.Sst)

    def mixer_half(self, l, hf):
        d = self.d
        w_in = d["w_in"][l]
        tiles = list(range(hf * 4, hf * 4 + 4))
        self.norm_T(tiles, 1, 0)
        self.attention(l, hf, w_in)
        self.merge_branch(l, 0, w_in, d["w_br_att"][l], self.yT128, 6, first=True)
        self.pool(l, hf, w_in)
        self.merge_branch(l, 1, w_in, d["w_br_pool"][l], self.yT96, 8, first=False)
        self.gla(l, hf, w_in)
        self.merge_branch(l, 2, w_in, d["w_br_gla"][l], self.yT96, 8, first=False)
        mT = self.hT[:, :, 512:1024]
        for dg in range(8):
            wv = self.wblock(d["w_out"][l], dg * 256, 256)
            for j in range(4):
                t = hf * 4 + j
                po = self.bank()
                for kc in range(16):
                    self.mm(po[:, 0:256], mT[:, kc, j * 128:(j + 1) * 128], wv[:, kc, :], start=(kc == 0), stop=(kc == 15))
                xd = self.x[:, t, dg * 256:(dg + 1) * 256]
                self.tt(xd, xd, po[:, 0:256], ALU.add)

    def final(self):
        self.tmp_reset()
        g_bc = self.ring[0][:, 0:4096].bitcast(F32)
        self.load(g_bc, self.pbc(self.d["norm_final"]))
        outs = [self.ring[1][:, 0:4096].bitcast(F32), self.ring[2][:, 0:4096].bitcast(F32)]
        sts = [self.tmp(F32, 4) for _ in range(2)]
        yv = self.o_y.rearrange("(t p) d -> p t d", p=128)
        for t in range(NT):
            st = sts[t % 2]
            o = outs[t % 2]
            save = self.tmp_top
            self.rstd_of(self.x[:, t, :], st)
            self.tmp_top = save
            self.stt(o, self.x[:, t, :], st[:, 2:3], g_bc, ALU.mult, ALU.mult)
            self.store(yv[:, t, :], o)

    def raw_out(self):
        yv = self.o_y.rearrange("(t p) d -> p t d", p=128)
        for t in range(NT):
            self.store(yv[:, t, :], self.x[:, t, :])

    def build(self):
        self.setup()
        d = self.d
        for l in range(self.n_layers):
            self.layer_params(l)
            self.norm_T(list(range(NT)), 0, 0)
            self.ffn(d["ffn1_wi"][l], d["ffn1_wo"][l])
            for hf in range(2):
                self.mixer_half(l, hf)
            self.norm_T(list(range(NT)), 2, 0)
            self.ffn(d["ffn2_wi"][l], d["ffn2_wo"][l])
        if self.final_norm:
            self.final()
        else:
            self.raw_out()
        self.S.emit()
        return self.nc


def _consts(first_half):
    bf = ml_dtypes.bfloat16
    c = {}
    c["ident"] = np.eye(128, dtype=np.float32).astype(bf)
    qi = np.arange(128)[:, None]
    si = np.arange(256)[None, :]
    rel = 128 + qi - si
    band = (rel >= 0) & (rel < 128)
    maskA = np.where(band, 0.0, NEG).astype(np.float32)
    c["maskA"] = maskA
    if first_half:
        c["mask0"] = np.where(band & (si >= 128), 0.0, NEG).astype(np.float32)
    else:
        c["mask0"] = maskA.copy()
    tp = np.arange(128)[:, None]
    tt = np.arange(128)[None, :]
    c["tri_incl"] = (tp <= tt).astype(np.float32).astype(bf)
    c["tri_su"] = (tp > tt).astype(np.float32).astype(bf)
    c["maskT"] = (tp <= tt).astype(np.float32).astype(bf)
    bands = np.zeros((128, 8, 128), np.float32)
    bands0 = np.zeros((128, 8, 128), np.float32)
    for g, w in enumerate((2, 4, 8, 16)):
        cur = ((tp <= tt) & (tp > tt - w)).astype(np.float32) / w - (tp == tt).astype(np.float32)
        prv = ((tp - 128) > (tt - w)).astype(np.float32) / w
        bands[:, g, :] = cur
        bands[:, 4 + g, :] = prv
        if first_half:
            cnt = np.minimum(tt + 1, w).astype(np.float32)
            bands0[:, g, :] = ((tp <= tt) & (tp > tt - w)).astype(np.float32) / cnt - (tp == tt).astype(np.float32)
        else:
            bands0[:, g, :] = cur
            bands0[:, 4 + g, :] = prv
    c["bands"] = bands.astype(bf)
    c["bands0"] = bands0.astype(bf)
    half = 32
    inv_freq = (10000.0 ** (-np.arange(half, dtype=np.float32) / half)).astype(np.float32)
    c["invf"] = np.broadcast_to(inv_freq[None, :], (128, 32)).copy()
    return c


_PROG_CACHE = {}


def _get_prog(n_layers=L_ALL, final_norm=True):
    key = (n_layers, final_norm)
    if key not in _PROG_CACHE:
        _PROG_CACHE[key] = Prog(n_layers, final_norm).build()
    return _PROG_CACHE[key]


def _shared_inputs(inp):
    f = np.float32
    sh = {}
    for k in ("ffn1_wi", "ffn1_wo", "ffn2_wi", "ffn2_wo", "w_in", "att_sinks", "w_pool", "w_gla_a2", "b_gla_a",
              "gla_norm", "w_br_att", "w_br_pool", "w_br_gla", "w_out"):
        sh[k] = np.ascontiguousarray(inp[k], dtype=f)
    for k in ("norm_ffn1", "norm_mix", "norm_ffn2"):
        sh[k + "_t"] = np.ascontiguousarray(np.asarray(inp[k], f).reshape(L_ALL, 16, 128).transpose(0, 2, 1))
    sh["norm_final"] = np.asarray(inp["norm_final"], f).reshape(1, D)
    sh["b_gate_t"] = np.ascontiguousarray(np.asarray(inp["b_gate"], f).reshape(L_ALL, 48, 128).transpose(0, 2, 1))
    sh["pool_scale_t"] = np.ascontiguousarray(np.asarray(inp["pool_scale"], f).reshape(L_ALL, 8, 96).transpose(0, 2, 1))
    return sh


def _core_inputs(inp, sh, b, half, carry):
    m = dict(sh)
    s0 = half * NTOK
    m["x"] = np.ascontiguousarray(np.asarray(inp["x"], np.float32)[b, s0:s0 + NTOK])
    pos = np.asarray(inp["positions"], np.int32)[b, s0:s0 + NTOK]
    m["pos"] = np.ascontiguousarray(pos.reshape(NT, 128).T)
    m.update(_consts(half == 0))
    bf = ml_dtypes.bfloat16
    if carry is None:
        m["c_kt"] = np.zeros((L_ALL, 64, 512), bf)
        m["c_v"] = np.zeros((L_ALL, 128, 256), bf)
        m["c_u"] = np.zeros((L_ALL, 128, 768), bf)
        m["c_s"] = np.zeros((L_ALL, 96, 768), np.float32)
    else:
        m["c_kt"], m["c_v"], m["c_u"], m["c_s"] = carry["o_kt"], carry["o_v"], carry["o_u"], carry["o_s"]
    return m


def kernel(**inp):
    nc = _get_prog()
    sh = _shared_inputs(inp)
    B = inp["x"].shape[0]
    out = np.zeros((B, 2 * NTOK, D), np.float32)
    maps = [_core_inputs(inp, sh, b, 0, None) for b in range(B)]
    r1 = run_bass_kernel_spmd(nc, maps, core_ids=list(range(B))).results
    for b in range(B):
        out[b, 0:NTOK] = r1[b]["y"]
    maps = [_core_inputs(inp, sh, b, 1, r1[b]) for b in range(B)]
    r2 = run_bass_kernel_spmd(nc, maps, core_ids=list(range(B))).results
    for b in range(B):
        out[b, NTOK:] = r2[b]["y"]
    return out
```

```python
import numpy as np
import ml_dtypes
import concourse.bass as bass
import concourse.mybir as mybir
from concourse.bass_utils import run_bass_kernel_spmd

F32 = mybir.dt.float32
BF16 = mybir.dt.bfloat16
I32 = mybir.dt.int32
U8 = mybir.dt.uint8
AF = mybir.ActivationFunctionType
ALU = mybir.AluOpType
AX = mybir.AxisListType
DTSZ = {F32: 4, BF16: 2, I32: 4, U8: 1}

D = 2048
NT = 8
NTOK = NT * 128
L_ALL = 4
NS = 5
CCW = 1536
DFF = 5632
NCH = DFF // 128
WIN = 10512
C_Q, C_K, C_V, C_PU = 0, 768, 1024, 1280
C_GQ, C_GK, C_GV, C_GO, C_LR, C_GATE = 2048, 2432, 2816, 3584, 4352, 4368
EPS = 1e-6
GRAN = 128
NEG = -30000.0


class Ins:
    __slots__ = ("eng", "fn", "deps", "sig", "val", "is_dma", "dsem", "dval", "dprev", "cc")

    def __init__(self, eng, fn, is_dma):
        self.eng = eng
        self.fn = fn
        self.deps = set()
        self.sig = False
        self.val = 0
        self.is_dma = is_dma
        self.dsem = None
        self.dval = 0
        self.dprev = None
        self.cc = False


class Sched:
    ENGS = ("pe", "act", "dve", "pool", "sp")

    def __init__(self, nc, ndma_sems=12):
        self.nc = nc
        self.lists = {e: [] for e in self.ENGS}
        self.track = {}
        self.gcache = {}
        self.eng_sem = {e: nc.alloc_semaphore("sem_" + e) for e in ("pe", "act", "dve", "pool")}
        self.dma_sems = {q: [nc.alloc_semaphore("dsem_%s_%d" % (q, i)) for i in range(ndma_sems)]
                         for q in ("sp", "pool")}
        self.dma_hist = {q: [] for q in ("sp", "pool")}
        self.stores = []

    def granules(self, ap):
        key = (ap.tensor.name, ap.offset, tuple(ap.ap), str(ap.dtype))
        g = self.gcache.get(key)
        if g is not None:
            return g
        sz = DTSZ[ap.dtype]
        dims = list(ap.ap)
        row = dims[0][0]
        off = ap.offset % row if row > 0 else ap.offset
        free = dims[1:]
        name = ap.tensor.name
        res = set()
        if not free:
            free = [(1, 1)]
        last = free[-1]
        outer = free[:-1]
        bases = [off]
        for (st, cnt) in outer:
            bases = [b + st * i for b in bases for i in range(cnt)]
        ls, lc = last
        span = 1 if ls == 0 else ls * (lc - 1) + 1
        for b in bases:
            lo = (b * sz) // GRAN
            hi = ((b + span) * sz - 1) // GRAN
            for gi in range(lo, hi + 1):
                res.add((name, gi))
        g = frozenset(res)
        self.gcache[key] = g
        return g

    def op(self, eng, fn, reads=(), writes=(), is_dma=False, store=False, cc=False):
        ins = Ins(eng, fn, is_dma)
        deps = ins.deps
        tr = self.track
        for ap in reads:
            for g in self.granules(ap):
                e = tr.get(g)
                if e is None:
                    tr[g] = [None, [ins]]
                else:
                    if e[0] is not None:
                        deps.add(e[0])
                    e[1].append(ins)
        for ap in writes:
            for g in self.granules(ap):
                e = tr.get(g)
                if e is None:
                    tr[g] = [ins, []]
                else:
                    if e[0] is not None:
                        deps.add(e[0])
                    for r in e[1]:
                        deps.add(r)
                    e[0] = ins
                    e[1] = []
        deps.discard(ins)
        if eng == "pe":
            ins.deps = deps = {d for d in deps if not (d.eng == "pe" and not d.is_dma)}
        for d in deps:
            d.sig = True
        if cc:
            ins.dsem = self.nc.alloc_semaphore("ccsem_%d" % len(self.lists[eng]))
            ins.dval = 1
            ins.cc = True
        elif is_dma:
            h = self.dma_hist[eng]
            j = len(h)
            sems = self.dma_sems[eng]
            k = len(sems)
            ins.dsem = sems[j % k]
            ins.dval = 16 * (j // k + 1)
            if j >= k:
                ins.dprev = h[j - k]
            h.append(ins)
            if store:
                self.stores.append(ins)
        self.lists[eng].append(ins)
        return ins

    def emit(self):
        nc = self.nc
        for e in self.ENGS:
            cnt = 0
            for ins in self.lists[e]:
                if not ins.is_dma and ins.sig:
                    cnt += 1
                    ins.val = cnt
        handles = {"pe": nc.tensor, "act": nc.scalar, "dve": nc.vector, "pool": nc.gpsimd, "sp": nc.sync}
        bnames = {"pe": "tensor", "act": "scalar", "dve": "vector", "pool": "gpsimd", "sp": "sync"}
        nwaits = {e: 0 for e in self.ENGS}

        def run_engine(e):
            h = handles[e]
            waited = {}

            def wait(sem, val):
                key = id(sem)
                if waited.get(key, 0) >= val:
                    return
                h.wait_ge(sem, val)
                waited[key] = val
                nwaits[e] += 1

            for ins in self.lists[e]:
                for d in ins.deps:
                    if d.is_dma:
                        wait(d.dsem, d.dval)
                    else:
                        wait(self.eng_sem[d.eng], d.val)
                if ins.is_dma and ins.dprev is not None:
                    wait(ins.dprev.dsem, ins.dprev.dval)
                inst = ins.fn()
                if ins.cc:
                    inst.then_inc(ins.dsem)
                elif ins.is_dma:
                    inst.then_inc(ins.dsem, 16)
                elif ins.sig:
                    inst.then_inc(self.eng_sem[e], 1)
            if e == "sp":
                for st in self.stores:
                    wait(st.dsem, st.dval)

        with nc.Block() as block:
            for e in self.ENGS:
                getattr(block, bnames[e])(lambda _h, _e=e: run_engine(_e))
        self.nwaits = nwaits


class Prog:
    def __init__(self, n_layers=NS, final_norm=True):
        self.n_layers = n_layers
        self.final_norm = final_norm
        nc = self.nc = bass.Bass("TRN2", target_bir_lowering=False)
        self.S = Sched(nc)
        L = NS
        dt = nc.dram_tensor

        def din(name, shape, dtype):
            return dt(name, list(shape), dtype, kind="ExternalInput").ap()

        def dout(name, shape, dtype):
            return dt(name, list(shape), dtype, kind="ExternalOutput").ap()

        self.d = {}
        for name, shape, dtype in [
            ("x", (NTOK, D), F32), ("pos", (128, NT), I32), ("invf", (128, 32), F32),
            ("norm_ffn1_t", (L, 128, 16), F32), ("norm_mix_t", (L, 128, 16), F32), ("norm_ffn2_t", (L, 128, 16), F32),
            ("norm_final", (1, D), F32),
            ("ffn1_wi", (L, D, 2 * DFF), F32), ("ffn1_wo", (L, DFF, D), F32),
            ("ffn2_wi", (L, D, 2 * DFF), F32), ("ffn2_wo", (L, DFF, D), F32),
            ("w_in", (L, D, WIN), F32), ("b_gate_t", (L, 128, 48), F32), ("att_sinks", (L, 12), F32),
            ("w_pool", (L, 4, 192, 192), F32), ("pool_scale_t", (L, 96, 8), F32),
            ("w_gla_a2", (L, 16, 384), F32), ("b_gla_a", (L, 384), F32), ("gla_norm", (L, 768), F32),
            ("w_br_att", (L, 768, D), F32), ("w_br_pool", (L, 768, D), F32), ("w_br_gla", (L, 768, D), F32),
            ("w_out", (L, D, D), F32),
            ("ident", (128, 128), BF16), ("maskA", (128, 256), F32), ("mask0", (128, 256), F32),
            ("tri_incl", (128, 128), BF16), ("tri_su", (128, 128), BF16), ("maskT", (128, 128), BF16),
            ("bands", (128, 8, 128), BF16), ("bands0", (128, 8, 128), BF16),
            ("czero", (128, CCW), F32), ("flag", (128, 1), F32),
        ]:
            self.d[name] = din(name, shape, dtype)
        self.o_y = dout("y", (NTOK, D), F32)
        self.cin = [dt("cin%d" % i, [128, CCW], F32).ap() for i in range(NS - 1)]
        self.cout = [dt("cout%d" % i, [256, CCW], F32).ap() for i in range(NS - 1)]

        self.arena_bytes = 212000
        self.A = nc.alloc_sbuf_tensor("arena", [128, self.arena_bytes], U8)
        self.top = 0
        al = self.alloc
        self.x = al(F32, NT, D)
        self.hT = al(BF16, 16, NTOK)
        self.ring = [al(BF16, 4096) for _ in range(4)]
        self.ring_i = 0
        self.ident = al(BF16, 128)
        self.maskA = al(F32, 256)
        self.mask0 = al(F32, 256)
        self.tri_incl = al(BF16, 128)
        self.tri_su = al(BF16, 128)
        self.maskT = al(BF16, 128)
        self.bands = al(BF16, 8, 128)
        self.bands0 = al(BF16, 8, 128)
        self.cos = al(F32, NT, 32)
        self.sin = al(F32, NT, 32)
        self.cst = al(F32, 8)
        self.stat = al(F32, 32)
        self.gains = al(F32, 3, 16)
        self.bgate = al(F32, 48)
        self.sinks = al(F32, 12)
        self.pscale = al(F32, 8, parts=96)
        self.ba_bc = al(F32, 384)
        self.gn_bc = al(F32, 768)
        self.w_a2 = al(BF16, 384, parts=16)
        self.w_pool = al(BF16, 8, 192, parts=96)
        self.KT = al(BF16, 4, 5, 128, parts=64)
        self.Vb = al(BF16, 5, 256)
        self.uh = al(BF16, 768)
        self.Sst = al(F32, 4, 192, parts=96)
        self.Sbf = al(BF16, 4, 192, parts=96)
        big_off = self.alloc_bytes(4 * 2320 * 2)
        self.p_tm = self.view(big_off, 128, BF16, 4, 2320)
        self.gT = self.view(big_off, 128, BF16, 8, NTOK)
        yt_off = self.alloc_bytes(8192)
        self.yT128 = self.view(yt_off, 128, BF16, 6, 512)
        self.yT96 = self.view(yt_off, 96, BF16, 8, 512)
        self.tmp_off = self.top
        self.tmp_top = self.top
        assert self.top <= self.arena_bytes, self.top
        self.banks = [nc.alloc_psum_tensor("ps%d" % i, [128, 512], F32) for i in range(8)]
        self.bank_i = 0

    def alloc_bytes(self, n, align=64):
        off = (self.top + align - 1) // align * align
        self.top = off + n
        assert self.top <= self.arena_bytes, ("SBUF overflow", self.top)
        return off

    def view(self, off, parts, dtype, *dims):
        n = int(np.prod(dims)) * DTSZ[dtype]
        ap = self.A[0:parts, off:off + n].bitcast(dtype)
        if len(dims) == 2:
            ap = ap.rearrange("p (a b) -> p a b", b=dims[1])
        elif len(dims) == 3:
            ap = ap.rearrange("p (a b c) -> p a b c", b=dims[1], c=dims[2])
        return ap

    def alloc(self, dtype, *dims, parts=128):
        off = self.alloc_bytes(int(np.prod(dims)) * DTSZ[dtype])
        return self.view(off, parts, dtype, *dims)

    def tmp_reset(self):
        self.tmp_top = self.tmp_off

    def tmp(self, dtype, *dims, parts=128):
        n = int(np.prod(dims)) * DTSZ[dtype]
        off = (self.tmp_top + 63) // 64 * 64
        self.tmp_top = off + n
        assert self.tmp_top <= self.arena_bytes, ("SBUF tmp overflow", self.tmp_top)
        return self.view(off, parts, dtype, *dims)

    def bank(self):
        b = self.banks[self.bank_i % 8]
        self.bank_i += 1
        return b

    def slot(self):
        s = self.ring[self.ring_i % 4]
        self.ring_i += 1
        return s

    def mm(self, out, lhsT, rhs, start=True, stop=True):
        nc = self.nc
        self.S.op("pe", lambda: nc.tensor.matmul(out, lhsT=lhsT, rhs=rhs, start=start, stop=stop),
                  reads=[lhsT, rhs], writes=[out])

    def tr(self, out, in_):
        nc = self.nc
        ident = self.ident
        self.S.op("pe", lambda: nc.tensor.transpose(out, in_, ident), reads=[in_, ident], writes=[out])

    def act(self, out, in_, func, bias=None, scale=None, accum=None):
        nc = self.nc
        kw = {}
        reads = [in_]
        if bias is not None:
            kw["bias"] = bias
            if not isinstance(bias, (int, float)):
                reads.append(bias)
        if scale is not None:
            kw["scale"] = scale
            if not isinstance(scale, (int, float)):
                reads.append(scale)
        writes = [out]
        if accum is not None:
            kw["accum_out"] = accum
            writes.append(accum)
        self.S.op("act", lambda: nc.scalar.activation(out=out, in_=in_, func=func, **kw), reads=reads, writes=writes)

    def tt(self, out, in0, in1, op, eng="dve"):
        nc = self.nc
        e = nc.vector if eng == "dve" else nc.gpsimd
        self.S.op(eng, lambda: e.tensor_tensor(out=out, in0=in0, in1=in1, op=op), reads=[in0, in1], writes=[out])

    def ts(self, out, in0, s1, op0, s2=None, op1=None, eng="dve"):
        nc = self.nc
        e = nc.vector if eng == "dve" else nc.gpsimd
        reads = [in0]
        if not isinstance(s1, (int, float)):
            reads.append(s1)
        if s2 is not None and not isinstance(s2, (int, float)):
            reads.append(s2)
        if op1 is None:
            f = lambda: e.tensor_scalar(out=out, in0=in0, scalar1=s1, scalar2=None, op0=op0)
        else:
            f = lambda: e.tensor_scalar(out=out, in0=in0, scalar1=s1, scalar2=s2, op0=op0, op1=op1)
        self.S.op(eng, f, reads=reads, writes=[out])

    def stt(self, out, in0, scalar, in1, op0, op1):
        nc = self.nc
        reads = [in0, in1]
        if not isinstance(scalar, (int, float)):
            reads.append(scalar)
        self.S.op("dve", lambda: nc.vector.scalar_tensor_tensor(out=out, in0=in0, scalar=scalar, in1=in1,
                                                                 op0=op0, op1=op1), reads=reads, writes=[out])

    def red(self, out, in_, op, axis=AX.X):
        nc = self.nc
        self.S.op("dve", lambda: nc.vector.tensor_reduce(out=out, in_=in_, axis=axis, op=op), reads=[in_], writes=[out])

    def cp(self, out, in_, eng="dve"):
        nc = self.nc
        if eng == "act":
            self.S.op("act", lambda: nc.scalar.copy(out=out, in_=in_), reads=[in_], writes=[out])
        else:
            e = nc.vector if eng == "dve" else nc.gpsimd
            self.S.op(eng, lambda: e.tensor_copy(out=out, in_=in_), reads=[in_], writes=[out])

    def recip(self, out, in_):
        nc = self.nc
        self.S.op("dve", lambda: nc.vector.reciprocal(out=out, in_=in_), reads=[in_], writes=[out])

    def memset(self, out, val):
        nc = self.nc
        self.S.op("dve", lambda: nc.vector.memset(out, val), reads=[], writes=[out])

    def load(self, out, in_, cast=False, tracked_src=False):
        nc = self.nc
        rd = [in_] if tracked_src else []
        if cast:
            self.S.op("pool", lambda: nc.gpsimd.dma_start(out=out, in_=in_), reads=rd, writes=[out], is_dma=True)
        else:
            self.S.op("sp", lambda: nc.sync.dma_start(out=out, in_=in_), reads=rd, writes=[out], is_dma=True)

    def dstore(self, out, in_):
        nc = self.nc
        self.S.op("sp", lambda: nc.sync.dma_start(out=out, in_=in_), reads=[in_], writes=[out], is_dma=True)

    def exchange(self, l):
        nc = self.nc
        cin, cout = self.cin[l], self.cout[l]
        self.S.op("pool", lambda: nc.gpsimd.collective_compute(
            "AllGather", ALU.bypass, replica_groups=[[0, 1], [2, 3], [4, 5], [6, 7]],
            ins=[cin.opt()], outs=[cout.opt()]), reads=[cin], writes=[cout], is_dma=True, cc=True)

    def carry_src(self, l):
        if l == 0:
            return self.d["czero"], False
        return self.cout[l - 1][0:128, :], True

    @staticmethod
    def pbc(row):
        return row.partition_broadcast(128)[:, 0, :]

    def store(self, out, in_):
        nc = self.nc
        self.S.op("sp", lambda: nc.sync.dma_start(out=out, in_=in_), reads=[in_], writes=[], is_dma=True, store=True)

    def wblock(self, w2d, c0, ncols, kparts=128, r0=0, nk=16):
        s = self.slot()
        v = s[0:kparts, 0:nk * ncols].rearrange("p (k n) -> p k n", n=ncols)
        src = w2d[r0:r0 + nk * kparts, c0:c0 + ncols].rearrange("(k p) n -> p k n", p=kparts)
        self.load(v, src, cast=True)
        return v

    def setup(self):
        d = self.d
        xv = d["x"].rearrange("(t p) d -> p t d", p=128)
        for t in range(NT):
            self.load(self.x[:, t, :], xv[:, t, :])
        for name, dst in [("ident", self.ident), ("maskA", self.maskA), ("mask0", self.mask0),
                          ("tri_incl", self.tri_incl), ("tri_su", self.tri_su), ("maskT", self.maskT),
                          ("bands", self.bands), ("bands0", self.bands0)]:
            self.load(dst, d[name])
        self.memset(self.cst[:, 0:1], EPS)
        self.memset(self.cst[:, 1:2], 1.0)
        self.memset(self.cst[:, 2:3], 0.0)
        self.load(self.cst[:, 3:4], d["flag"])
        self.tmp_reset()
        posi = self.tmp(I32, NT)
        invf = self.tmp(F32, 32)
        posf = self.tmp(F32, NT)
        ang = self.tmp(F32, NT, 32)
        a2 = self.tmp(F32, NT, 32)
        kq = self.tmp(F32, NT, 32)
        ki = self.tmp(I32, NT, 32)
        kf = self.tmp(F32, NT, 32)
        m = self.tmp(F32, NT, 32)
        self.load(posi, d["pos"])
        self.load(invf, d["invf"])
        self.cp(posf, posi)
        self.tt(ang, posf.unsqueeze(2).broadcast_to([128, NT, 32]), invf.unsqueeze(1).broadcast_to([128, NT, 32]), ALU.mult)
        TWO_PI = 2.0 * np.pi
        C1 = 6.28125
        C2 = TWO_PI - C1
        for shift, dst in [(0.0, self.sin), (np.pi / 2, self.cos)]:
            if shift != 0.0:
                self.ts(a2, ang, float(shift), ALU.add)
                a = a2
            else:
                a = ang
            self.ts(kq, a, float(1.0 / TWO_PI), ALU.mult)
            self.cp(ki, kq)
            self.cp(kf, ki)
            self.stt(m, kf, float(-C1), a, ALU.mult, ALU.add)
            self.stt(m, kf, float(-C2), m, ALU.mult, ALU.add)
            self.ts(kq, m, float(np.pi), ALU.is_gt, float(-TWO_PI), ALU.mult)
            self.tt(m, m, kq, ALU.add)
            self.ts(kq, m, float(-np.pi), ALU.is_lt, float(TWO_PI), ALU.mult)
            self.tt(m, m, kq, ALU.add)
            self.ts(m, m, 3.1415925, ALU.min, -3.1415925, ALU.max)
            self.act(dst, m, AF.Sin)

    def layer_params(self, l):
        d = self.d
        self.load(self.gains[:, 0, :], d["norm_ffn1_t"][l])
        self.load(self.gains[:, 1, :], d["norm_mix_t"][l])
        self.load(self.gains[:, 2, :], d["norm_ffn2_t"][l])
        self.load(self.bgate, d["b_gate_t"][l])
        self.load(self.sinks, self.pbc(d["att_sinks"][l:l + 1, :]))
        self.load(self.pscale, d["pool_scale_t"][l])
        self.load(self.ba_bc, self.pbc(d["b_gla_a"][l:l + 1, :]))
        self.load(self.gn_bc, self.pbc(d["gla_norm"][l:l + 1, :]))
        self.load(self.w_a2, d["w_gla_a2"][l], cast=True)
        self.load(self.w_pool, d["w_pool"][l].rearrange("g (i p) d -> p (g i) d", p=96), cast=True)

    def rstd_of(self, xt, st):
        junk = self.tmp(BF16, D)
        self.act(junk, xt, AF.Square, accum=st[:, 0:1])
        self.act(st[:, 1:2], st[:, 0:1], AF.Ln, scale=1.0 / D, bias=self.cst[:, 0:1])
        self.act(st[:, 2:3], st[:, 1:2], AF.Exp, scale=-0.5)

    def norm_T(self, tiles, gi, col0):
        self.tmp_reset()
        sts = [self.tmp(F32, 4) for _ in range(2)]
        xss = [self.tmp(BF16, D) for _ in range(2)]
        gain = self.gains[:, gi, :]
        for j, t in enumerate(tiles):
            st = sts[j % 2]
            xs = xss[j % 2]
            save = self.tmp_top
            self.rstd_of(self.x[:, t, :], st)
            self.tmp_top = save
            self.ts(xs, self.x[:, t, :], st[:, 2:3], ALU.mult)
            for hb in range(2):
                pb = self.bank()[:, 0:512].bitcast(BF16).rearrange("p (k n) -> p k n", n=128)
                for k in range(8):
                    kc = hb * 8 + k
                    self.tr(pb[:, k, :], xs[:, kc * 128:(kc + 1) * 128])
                c = col0 + j * 128
                self.tt(self.hT[:, hb * 8:(hb + 1) * 8, c:c + 128], pb,
                        gain[:, hb * 8:(hb + 1) * 8].unsqueeze(2).broadcast_to([128, 8, 128]), ALU.mult)

    def ffn(self, wi, wo):
        self.tmp_reset()
        sil = [self.tmp(F32, 512) for _ in range(2)]
        si = 0
        rounds = [(c0, min(8, NCH - c0)) for c0 in range(0, NCH, 8)]
        for (c0, nchk) in rounds:
            for cp2 in range(0, nchk, 2):
                ch0 = c0 + cp2
                wa = self.wblock(wi, ch0 * 128, 256)
                wb = self.wblock(wi, DFF + ch0 * 128, 256)
                for cc in range(2):
                    lc = cp2 + cc
                    for th in range(NTOK // 512):
                        pa = self.bank()
                        pb = self.bank()
                        for kc in range(16):
                            self.mm(pa[:, :], wa[:, kc, cc * 128:(cc + 1) * 128], self.hT[:, kc, th * 512:(th + 1) * 512],
                                    start=(kc == 0), stop=(kc == 15))
                        for kc in range(16):
                            self.mm(pb[:, :], wb[:, kc, cc * 128:(cc + 1) * 128], self.hT[:, kc, th * 512:(th + 1) * 512],
                                    start=(kc == 0), stop=(kc == 15))
                        s = sil[si % 2]
                        si += 1
                        self.act(s, pa[:, :], AF.Silu)
                        self.tt(self.gT[:, lc, th * 512:(th + 1) * 512], s, pb[:, :], ALU.mult)
            for dg in range(4):
                wv = self.wblock(wo, dg * 512, 512, r0=c0 * 128, nk=nchk)
                for t in range(NT):
                    po = self.bank()
                    for lc in range(nchk):
                        self.mm(po[:, :], self.gT[:, lc, t * 128:(t + 1) * 128], wv[:, lc, :],
                                start=(lc == 0), stop=(lc == nchk - 1))
                    xd = self.x[:, t, dg * 512:(dg + 1) * 512]
                    self.stt(xd, po[:, :], 0.5, xd, ALU.mult, ALU.add)

    def project(self, w_in, hf, c_start, c_end, base_col, evac):
        c = c_start
        while c < c_end:
            n = min(256, c_end - c)
            wv = self.wblock(w_in, c, n)
            for j in range(4):
                ps = self.bank()
                for kc in range(16):
                    self.mm(ps[:, 0:n], self.hT[:, kc, j * 128:(j + 1) * 128], wv[:, kc, :],
                            start=(kc == 0), stop=(kc == 15))
                evac(j, ps[:, 0:n], c - base_col, c - base_col + n)
            c += n

    def merge_branch(self, l, bi, w_in, w_br, yT, nyc, first):
        self.tmp_reset()
        mT = self.hT[:, :, 512:1024]
        sig = [self.tmp(F32, 512) for _ in range(2)]
        tmpz = [self.tmp(F32, 512) for _ in range(2)]
        kp = yT.shape[0]
        i = 0
        for ng in range(4):
            wgs = [self.wblock(w_in, C_GATE + bi * D + ng * 512 + q * 256, 256) for q in range(2)]
            wbr = self.wblock(w_br, ng * 512, 512, kparts=kp, nk=nyc)
            for nn in range(4):
                n = ng * 4 + nn
                wg = wgs[nn // 2]
                cc = (nn % 2) * 128
                pg = self.bank()
                pz = self.bank()
                for kc in range(16):
                    self.mm(pg[:, :], wg[:, kc, cc:cc + 128], self.hT[:, kc, 0:512], start=(kc == 0), stop=(kc == 15))
                for c in range(nyc):
                    self.mm(pz[:, :], wbr[:, c, nn * 128:(nn + 1) * 128], yT[:, c, :], start=(c == 0), stop=(c == nyc - 1))
                sg = sig[i % 2]
                tz = tmpz[i % 2]
                i += 1
                self.act(sg, pg[:, :], AF.Sigmoid, bias=self.bgate[:, bi * 16 + n:bi * 16 + n + 1])
                if first:
                    self.tt(mT[:, n, :], sg, pz[:, :], ALU.mult)
                else:
                    self.tt(tz, sg, pz[:, :], ALU.mult)
                    self.tt(mT[:, n, :], tz, mT[:, n, :], ALU.add)

    def attention(self, l, hf, w_in):
        self.tmp_reset()
        p_tm = self.p_tm
        cosv, sinv = self.cos, self.sin

        def evac(j, ps, lo, hi):
            t = hf * 4 + j
            if lo >= C_V:
                self.cp(self.Vb[:, j + 1, lo - C_V:hi - C_V], ps, eng="act")
                return
            nh = (hi - lo) // 64
            src = ps.rearrange("p (h two f) -> p h two f", two=2, f=32)
            dst = p_tm[:, j, lo:hi].rearrange("p (h two f) -> p h two f", two=2, f=32)
            cb = cosv[:, t, :].unsqueeze(1).broadcast_to([128, nh, 32])
            sb = sinv[:, t, :].unsqueeze(1).broadcast_to([128, nh, 32])
            t1 = self.rt1[:, 0:nh, :]
            t2 = self.rt2[:, 0:nh, :]
            x1 = src[:, :, 0, :]
            x2 = src[:, :, 1, :]
            self.tt(t1, x1, cb, ALU.mult)
            self.tt(t2, x2, sb, ALU.mult)
            self.tt(dst[:, :, 0, :], t1, t2, ALU.subtract)
            self.tt(t1, x2, cb, ALU.mult)
            self.tt(t2, x1, sb, ALU.mult)
            self.tt(dst[:, :, 1, :], t1, t2, ALU.add)

        self.rt1 = self.tmp(F32, 4, 32)
        self.rt2 = self.tmp(F32, 4, 32)
        if hf == 0:
            src, trk = self.carry_src(l)
            self.load(self.KT[:, :, 0, :], src[0:64, 0:256].bitcast(BF16).rearrange("p (h n) -> p h n", n=128), tracked_src=trk)
            self.load(self.Vb[:, 0, :], src[:, 256:384].bitcast(BF16), tracked_src=trk)
        else:
            self.cp(self.KT[:, :, 0, :], self.KT[:, :, 4, :], eng="act")
            self.cp(self.Vb[:, 0, :], self.Vb[:, 4, :], eng="act")
        self.project(w_in, hf, 0, C_PU, 0, evac)
        QT = [self.tmp(BF16, 12, 128, parts=64) for _ in range(2)]
        sm = [self.tmp(F32, 2, 256) for _ in range(2)]
        ee = [self.tmp(BF16, 2, 256) for _ in range(2)]
        PT = [self.tmp(BF16, 4, 128) for _ in range(2)]
        yat = [self.tmp(BF16, 768) for _ in range(2)]
        stt_ = [self.tmp(F32, 16) for _ in range(2)]
        pi = 0
        for j in range(4):
            t = hf * 4 + j
            qt = QT[j % 2]
            ya = yat[j % 2]
            pk = self.bank()[0:64, 0:256].bitcast(BF16).rearrange("p (h n) -> p h n", n=128)
            for kh in range(4):
                self.tr(pk[:, kh, :], p_tm[:, j, C_K + kh * 64:C_K + (kh + 1) * 64])
            self.cp(self.KT[:, :, j + 1, :], pk, eng="act")
            for qb in range(2):
                pq = self.bank()[0:64, 0:384].bitcast(BF16).rearrange("p (h n) -> p h n", n=128)
                for hh in range(6):
                    h = qb * 6 + hh
                    self.tr(pq[:, hh, :], p_tm[:, j, h * 64:(h + 1) * 64])
                self.cp(qt[:, qb * 6:(qb + 1) * 6, :], pq, eng="act")
            mask = self.mask0 if (hf == 0 and j == 0) else self.maskA
            for hp in range(6):
                s_m = sm[pi % 2]
                e_ = ee[pi % 2]
                pt = PT[pi % 2]
                st = stt_[pi % 2]
                pi += 1
                ps = self.bank()
                psv = ps[:, :].rearrange("p (a n) -> p a n", n=256)
                for a in range(2):
                    h = hp * 2 + a
                    kh = h // 3
                    self.mm(psv[:, a, :], qt[:, h, :], self.KT[:, kh, j:j + 2, :].rearrange("p a n -> p (a n)"), start=True, stop=True)
                self.stt(s_m, psv, 0.125, mask.unsqueeze(1).broadcast_to([128, 2, 256]), ALU.mult, ALU.add)
                self.red(st[:, 0:2], s_m, ALU.max)
                self.tt(st[:, 0:2], st[:, 0:2], self.sinks[:, hp * 2:hp * 2 + 2], ALU.max)
                self.ts(st[:, 2:4], st[:, 0:2], -1.0, ALU.mult)
                for a in range(2):
                    self.act(e_[:, a, :], s_m[:, a, :], AF.Exp, bias=st[:, 2 + a:3 + a], accum=st[:, 4 + a:5 + a])
                self.tt(st[:, 6:8], self.sinks[:, hp * 2:hp * 2 + 2], st[:, 2:4], ALU.add)
                self.act(st[:, 8:10], st[:, 6:8], AF.Exp)
                self.tt(st[:, 10:12], st[:, 4:6], st[:, 8:10], ALU.add)
                self.recip(st[:, 12:14], st[:, 10:12])
                pp = self.bank()[:, 0:256].bitcast(BF16).rearrange("p (a n) -> p a n", n=128)
                for a in range(2):
                    for kb in range(2):
                        self.tr(pp[:, a * 2 + kb, :], e_[:, a, kb * 128:(kb + 1) * 128])
                self.cp(pt, pp, eng="act")
                po = self.bank()
                for a in range(2):
                    h = hp * 2 + a
                    kh = h // 3
                    for kb in range(2):
                        self.mm(po[:, a * 64:(a + 1) * 64], pt[:, a * 2 + kb, :],
                                self.Vb[:, j + kb, kh * 64:(kh + 1) * 64], start=(kb == 0), stop=(kb == 1))
                for a in range(2):
                    h = hp * 2 + a
                    self.ts(ya[:, h * 64:(h + 1) * 64], po[:, a * 64:(a + 1) * 64], st[:, 12 + a:13 + a], ALU.mult)
            py = self.bank()[:, 0:384].bitcast(BF16).rearrange("p (c n) -> p c n", n=128)
            for c in range(6):
                self.tr(py[:, c, :], ya[:, c * 128:(c + 1) * 128])
            self.cp(self.yT128[:, :, j * 128:(j + 1) * 128], py, eng="act")
        if hf == 1 and l < NS - 1:
            self.dstore(self.cin[l][0:64, 0:256].bitcast(BF16).rearrange("p (h n) -> p h n", n=128), self.KT[:, :, 4, :])
            self.dstore(self.cin[l][:, 256:384].bitcast(BF16), self.Vb[:, 4, :])

    def pool(self, l, hf, w_in):
        self.tmp_reset()
        p_tm = self.p_tm

        def evac(j, ps, lo, hi):
            self.cp(p_tm[:, j, lo:hi], ps, eng="act")

        if hf == 0:
            src, trk = self.carry_src(l)
            self.load(self.uh, src[:, 384:768].bitcast(BF16), tracked_src=trk)
        self.project(w_in, hf, C_PU, C_GQ, C_PU, evac)
        dTs = [self.tmp(BF16, 8, 128, parts=96) for _ in range(2)]
        for j in range(4):
            dT = dTs[j % 2]
            first = (hf == 0 and j == 0)
            bands = self.bands0 if first else self.bands
            prev = self.uh if j == 0 else p_tm[:, j - 1, 0:768]
            cur = p_tm[:, j, 0:768]
            for half in range(2):
                pd = self.bank()[0:96, :].rearrange("p (c n) -> p c n", n=128)
                for cc in range(4):
                    c = half * 4 + cc
                    g = c // 2
                    self.mm(pd[:, cc, :], cur[:, c * 96:(c + 1) * 96], bands[:, g, :], start=True, stop=False)
                    self.mm(pd[:, cc, :], prev[:, c * 96:(c + 1) * 96], bands[:, 4 + g, :], start=False, stop=True)
                self.cp(dT[:, half * 4:(half + 1) * 4, :], pd, eng="act")
            for half in range(2):
                py = self.bank()[0:96, :].rearrange("p (c n) -> p c n", n=128)
                for cc in range(4):
                    c = half * 4 + cc
                    g, oc = c // 2, c % 2
                    for ic in range(2):
                        self.mm(py[:, cc, :], self.w_pool[:, g * 2 + ic, oc * 96:(oc + 1) * 96], dT[:, g * 2 + ic, :],
                                start=(ic == 0), stop=(ic == 1))
                self.tt(self.yT96[:, half * 4:(half + 1) * 4, j * 128:(j + 1) * 128], py,
                        self.pscale[:, half * 4:(half + 1) * 4].unsqueeze(2).broadcast_to([96, 4, 128]), ALU.mult)
        self.cp(self.uh, p_tm[:, 3, 0:768], eng="act")
        if hf == 1 and l < NS - 1:
            self.dstore(self.cin[l][:, 384:768].bitcast(BF16), self.uh)

    def gla(self, l, hf, w_in):
        self.tmp_reset()
        p_tm = self.p_tm
        G0 = C_GQ

        def evac(j, ps, lo, hi):
            og_lo, og_hi = C_GO - G0, C_LR - G0
            a, b = max(lo, og_lo), min(hi, og_hi)
            if a < b:
                self.act(p_tm[:, j, a:b], ps[:, a - lo:b - lo], AF.Silu)
            if lo < og_lo:
                b2 = min(hi, og_lo)
                self.cp(p_tm[:, j, lo:b2], ps[:, 0:b2 - lo], eng="act")
            if hi > og_hi:
                a2 = max(lo, og_hi)
                self.cp(p_tm[:, j, a2:hi], ps[:, a2 - lo:hi - lo], eng="act")

        if hf == 0:
            src, trk = self.carry_src(l)
            self.load(self.Sst, src[0:96, 768:1536].rearrange("p (h v) -> p h v", v=192), tracked_src=trk)
            self.ts(self.Sst, self.Sst, self.cst[0:96, 3:4], ALU.mult)
            self.cp(self.Sbf, self.Sst, eng="act")
        self.project(w_in, hf, C_GQ, C_GATE, C_GQ, evac)
        Q0, K0, V0, O0, R0 = 0, C_GK - G0, C_GV - G0, C_GO - G0, C_LR - G0
        lrT = self.tmp(BF16, 128, parts=16)
        zb = self.tmp(F32, 384)
        ax = self.tmp(F32, 384)
        mn = self.tmp(F32, 384)
        gk = self.tmp(BF16, 384)
        eb = self.tmp(F32, 384)
        enb = self.tmp(F32, 384)
        er = self.tmp(F32, 384)
        qt = self.tmp(BF16, 384)
        kt = self.tmp(BF16, 384)
        ks = self.tmp(BF16, 384)
        qkT = self.tmp(BF16, 8, 128, parts=96)
        ATm = self.tmp(BF16, 4, 128)
        dec = self.tmp(F32, 4, parts=96)
        st = self.tmp(F32, 16)
        otmp = self.tmp(F32, 768)
        yg = self.tmp(BF16, 768)
        onesc = self.tri_incl[:, 127:128]
        for j in range(4):
            pt = self.bank()[0:16, 0:64].bitcast(BF16)
            self.tr(pt, p_tm[:, j, R0:R0 + 16])
            self.cp(lrT, pt, eng="act")
            pz = self.bank()
            self.mm(pz[:, 0:384], lrT, self.w_a2, start=True, stop=True)
            self.tt(zb, pz[:, 0:384], self.ba_bc, ALU.add)
            self.stt(ax, zb, -1.0, zb, ALU.mult, ALU.max)
            self.act(ax, ax, AF.Exp, scale=-1.0)
            self.act(ax, ax, AF.Ln, bias=self.cst[:, 1:2])
            self.ts(mn, zb, 0.0, ALU.min)
            self.stt(mn, ax, -1.0, mn, ALU.mult, ALU.add)
            self.ts(gk, mn, 1.0 / 16.0, ALU.mult)
            pb_ = self.bank()
            pr_ = self.bank()
            self.mm(pb_[:, 0:384], self.tri_incl, gk, start=True, stop=True)
            self.mm(pr_[:, 0:384], self.tri_su, gk, start=True, stop=True)
            pd_ = self.bank()
            for h in range(4):
                self.mm(pd_[0:96, h:h + 1], gk[:, h * 96:(h + 1) * 96], onesc, start=True, stop=True)
            self.act(eb, pb_[:, 0:384], AF.Exp)
            self.act(enb, pb_[:, 0:384], AF.Exp, scale=-1.0)
            self.act(er, pr_[:, 0:384], AF.Exp)
            self.act(dec, pd_[0:96, 0:4], AF.Exp)
            self.stt(qt, p_tm[:, j, Q0:Q0 + 384], float(96 ** -0.5), eb, ALU.mult, ALU.mult)
            self.tt(kt, p_tm[:, j, K0:K0 + 384], enb, ALU.mult)
            self.tt(ks, p_tm[:, j, K0:K0 + 384], er, ALU.mult)
            pq = self.bank()[0:96, :].bitcast(BF16).rearrange("p (c n) -> p c n", n=128)
            for h in range(4):
                self.tr(pq[:, h, :], qt[:, h * 96:(h + 1) * 96])
                self.tr(pq[:, 4 + h, :], kt[:, h * 96:(h + 1) * 96])
            self.cp(qkT, pq[:, 0:8, :], eng="act")
            pa = self.bank()[:, :].rearrange("p (h n) -> p h n", n=128)
            for h in range(4):
                self.mm(pa[:, h, :], qkT[:, 4 + h, :], qkT[:, h, :], start=True, stop=True)
            self.tt(ATm, pa, self.maskT.unsqueeze(1).broadcast_to([128, 4, 128]), ALU.mult)
            po = [self.bank(), self.bank()]
            for h in range(4):
                o = po[h // 2][:, (h % 2) * 192:(h % 2 + 1) * 192]
                self.mm(o, ATm[:, h, :], p_tm[:, j, V0 + h * 192:V0 + (h + 1) * 192], start=True, stop=False)
                self.mm(o, qkT[:, h, :], self.Sbf[:, h, :], start=False, stop=True)
            pS = [self.bank(), self.bank()]
            for h in range(4):
                dS = pS[h // 2][0:96, (h % 2) * 192:(h % 2 + 1) * 192]
                self.mm(dS, ks[:, h * 96:(h + 1) * 96], p_tm[:, j, V0 + h * 192:V0 + (h + 1) * 192], start=True, stop=True)
            for h in range(4):
                dS = pS[h // 2][0:96, (h % 2) * 192:(h % 2 + 1) * 192]
                self.stt(self.Sst[:, h, :], self.Sst[:, h, :], dec[:, h:h + 1], dS, ALU.mult, ALU.add)
            self.cp(self.Sbf, self.Sst, eng="act")
            for h in range(4):
                o = po[h // 2][:, (h % 2) * 192:(h % 2 + 1) * 192]
                self.act(otmp[:, h * 192:(h + 1) * 192], o, AF.Square, accum=st[:, h:h + 1])
            self.act(st[:, 4:8], st[:, 0:4], AF.Ln, scale=1.0 / 192.0, bias=self.cst[:, 0:1])
            self.act(st[:, 8:12], st[:, 4:8], AF.Exp, scale=-0.5)
            for h in range(4):
                o = po[h // 2][:, (h % 2) * 192:(h % 2 + 1) * 192]
                self.stt(otmp[:, h * 192:(h + 1) * 192], o, st[:, 8 + h:9 + h], self.gn_bc[:, h * 192:(h + 1) * 192],
                         ALU.mult, ALU.mult)
            self.tt(yg, otmp, p_tm[:, j, O0:O0 + 768], ALU.mult)
            py = self.bank()[0:96, :].bitcast(BF16).rearrange("p (c n) -> p c n", n=128)
            for c in range(8):
                self.tr(py[:, c, :], yg[:, c * 96:(c + 1) * 96])
            self.cp(self.yT96[:, :, j * 128:(j + 1) * 128], py[:, 0:8, :], eng="act")
        if hf == 1 and l < NS - 1:
            self.dstore(self.cin[l][0:96, 768:1536].rearrange("p (h v) -> p h v", v=192), self.Sst)
            self.exchange(l)

    def mixer_half(self, l, hf):
        d = self.d
        w_in = d["w_in"][l]
        tiles = list(range(hf * 4, hf * 4 + 4))
        self.norm_T(tiles, 1, 0)
        self.attention(l, hf, w_in)
        self.merge_branch(l, 0, w_in, d["w_br_att"][l], self.yT128, 6, first=True)
        self.pool(l, hf, w_in)
        self.merge_branch(l, 1, w_in, d["w_br_pool"][l], self.yT96, 8, first=False)
        self.gla(l, hf, w_in)
        self.merge_branch(l, 2, w_in, d["w_br_gla"][l], self.yT96, 8, first=False)
        mT = self.hT[:, :, 512:1024]
        for dg in range(8):
            wv = self.wblock(d["w_out"][l], dg * 256, 256)
            for j in range(4):
                t = hf * 4 + j
                po = self.bank()
                for kc in range(16):
                    self.mm(po[:, 0:256], mT[:, kc, j * 128:(j + 1) * 128], wv[:, kc, :], start=(kc == 0), stop=(kc == 15))
                xd = self.x[:, t, dg * 256:(dg + 1) * 256]
                self.tt(xd, xd, po[:, 0:256], ALU.add)

    def final(self):
        self.tmp_reset()
        g_bc = self.ring[0][:, 0:4096].bitcast(F32)
        self.load(g_bc, self.pbc(self.d["norm_final"]))
        outs = [self.ring[1][:, 0:4096].bitcast(F32), self.ring[2][:, 0:4096].bitcast(F32)]
        sts = [self.tmp(F32, 4) for _ in range(2)]
        yv = self.o_y.rearrange("(t p) d -> p t d", p=128)
        for t in range(NT):
            st = sts[t % 2]
            o = outs[t % 2]
            save = self.tmp_top
            self.rstd_of(self.x[:, t, :], st)
            self.tmp_top = save
            self.stt(o, self.x[:, t, :], st[:, 2:3], g_bc, ALU.mult, ALU.mult)
            self.store(yv[:, t, :], o)

    def raw_out(self):
        yv = self.o_y.rearrange("(t p) d -> p t d", p=128)
        for t in range(NT):
            self.store(yv[:, t, :], self.x[:, t, :])

    def build(self):
        self.setup()
        d = self.d
        for l in range(self.n_layers):
            self.layer_params(l)
            self.norm_T(list(range(NT)), 0, 0)
            self.ffn(d["ffn1_wi"][l], d["ffn1_wo"][l])
            for hf in range(2):
                self.mixer_half(l, hf)
            self.norm_T(list(range(NT)), 2, 0)
            self.ffn(d["ffn2_wi"][l], d["ffn2_wo"][l])
        if self.final_norm:
            self.final()
        else:
            self.raw_out()
        self.S.emit()
        return self.nc


def _consts(first_half):
    bf = ml_dtypes.bfloat16
    c = {}
    c["ident"] = np.eye(128, dtype=np.float32).astype(bf)
    qi = np.arange(128)[:, None]
    si = np.arange(256)[None, :]
    rel = 128 + qi - si
    band = (rel >= 0) & (rel < 128)
    maskA = np.where(band, 0.0, NEG).astype(np.float32)
    c["maskA"] = maskA
    if first_half:
        c["mask0"] = np.where(band & (si >= 128), 0.0, NEG).astype(np.float32)
    else:
        c["mask0"] = maskA.copy()
    tp = np.arange(128)[:, None]
    tt = np.arange(128)[None, :]
    c["tri_incl"] = (tp <= tt).astype(np.float32).astype(bf)
    c["tri_su"] = (tp > tt).astype(np.float32).astype(bf)
    c["maskT"] = (tp <= tt).astype(np.float32).astype(bf)
    bands = np.zeros((128, 8, 128), np.float32)
    bands0 = np.zeros((128, 8, 128), np.float32)
    for g, w in enumerate((2, 4, 8, 16)):
        cur = ((tp <= tt) & (tp > tt - w)).astype(np.float32) / w - (tp == tt).astype(np.float32)
        prv = ((tp - 128) > (tt - w)).astype(np.float32) / w
        bands[:, g, :] = cur
        bands[:, 4 + g, :] = prv
        if first_half:
            cnt = np.minimum(tt + 1, w).astype(np.float32)
            bands0[:, g, :] = ((tp <= tt) & (tp > tt - w)).astype(np.float32) / cnt - (tp == tt).astype(np.float32)
        else:
            bands0[:, g, :] = cur
            bands0[:, 4 + g, :] = prv
    c["bands"] = bands.astype(bf)
    c["bands0"] = bands0.astype(bf)
    half = 32
    inv_freq = (10000.0 ** (-np.arange(half, dtype=np.float32) / half)).astype(np.float32)
    c["invf"] = np.broadcast_to(inv_freq[None, :], (128, 32)).copy()
    return c


_PROG_CACHE = {}


def _get_prog(n_layers=NS, final_norm=True):
    key = (n_layers, final_norm)
    if key not in _PROG_CACHE:
        _PROG_CACHE[key] = Prog(n_layers, final_norm).build()
    return _PROG_CACHE[key]


def _slotted(a, role):
    a = np.asarray(a, np.float32)
    out = np.zeros((NS,) + a.shape[1:], np.float32)
    out[role:role + L_ALL] = a
    return out


def _role_inputs(inp, role):
    f = np.float32
    sh = {}
    for k in ("ffn1_wi", "ffn1_wo", "ffn2_wi", "ffn2_wo", "w_in", "att_sinks", "w_pool", "w_gla_a2", "b_gla_a",
              "gla_norm", "w_br_att", "w_br_pool", "w_br_gla", "w_out"):
        sh[k] = _slotted(inp[k], role)
    for k in ("norm_ffn1", "norm_mix", "norm_ffn2"):
        sh[k + "_t"] = _slotted(np.asarray(inp[k], f).reshape(L_ALL, 16, 128).transpose(0, 2, 1), role)
    sh["norm_final"] = np.asarray(inp["norm_final"], f).reshape(1, D)
    sh["b_gate_t"] = _slotted(np.asarray(inp["b_gate"], f).reshape(L_ALL, 48, 128).transpose(0, 2, 1), role)
    sh["pool_scale_t"] = _slotted(np.asarray(inp["pool_scale"], f).reshape(L_ALL, 8, 96).transpose(0, 2, 1), role)
    sh.update(_consts(role == 0))
    sh["czero"] = np.zeros((128, CCW), f)
    sh["flag"] = np.full((128, 1), float(role), f)
    return sh


def _core_inputs(inp, sh, b, half):
    m = dict(sh)
    s0 = half * NTOK
    m["x"] = np.ascontiguousarray(np.asarray(inp["x"], np.float32)[b, s0:s0 + NTOK])
    pos = np.asarray(inp["positions"], np.int32)[b, s0:s0 + NTOK]
    m["pos"] = np.ascontiguousarray(pos.reshape(NT, 128).T)
    return m


def kernel(**inp):
    nc = _get_prog()
    roles = [_role_inputs(inp, 0), _role_inputs(inp, 1)]
    B = inp["x"].shape[0]
    maps = [_core_inputs(inp, roles[c % 2], c // 2, c % 2) for c in range(2 * B)]
    res = run_bass_kernel_spmd(nc, maps, core_ids=list(range(2 * B))).results
    out = np.zeros((B, 2 * NTOK, D), np.float32)
    for c in range(2 * B):
        out[c // 2, (c % 2) * NTOK:(c % 2 + 1) * NTOK] = res[c]["y"]
    return out
```

```python
import numpy as np
import ml_dtypes
import concourse.bass as bass
import concourse.mybir as mybir
from concourse.bass_utils import run_bass_kernel_spmd

F32 = mybir.dt.float32
BF16 = mybir.dt.bfloat16
I32 = mybir.dt.int32
U8 = mybir.dt.uint8
AF = mybir.ActivationFunctionType
ALU = mybir.AluOpType
AX = mybir.AxisListType
DTSZ = {F32: 4, BF16: 2, I32: 4, U8: 1}

D = 2048
NT = 8
NTOK = NT * 128
L_ALL = 4
NS = 5
CCW = 1536
DFF = 5632
NCH = DFF // 128
WIN = 10512
C_Q, C_K, C_V, C_PU = 0, 768, 1024, 1280
C_GQ, C_GK, C_GV, C_GO, C_LR, C_GATE = 2048, 2432, 2816, 3584, 4352, 4368
EPS = 1e-6
GRAN = 128
NEG = -30000.0


class Ins:
    __slots__ = ("eng", "fn", "deps", "sig", "val", "is_dma", "dsem", "dval", "dprev", "cc")

    def __init__(self, eng, fn, is_dma):
        self.eng = eng
        self.fn = fn
        self.deps = set()
        self.sig = False
        self.val = 0
        self.is_dma = is_dma
        self.dsem = None
        self.dval = 0
        self.dprev = None
        self.cc = False


class Sched:
    ENGS = ("pe", "act", "dve", "pool", "sp")

    def __init__(self, nc, ndma_sems=12):
        self.nc = nc
        self.lists = {e: [] for e in self.ENGS}
        self.track = {}
        self.gcache = {}
        self.eng_sem = {e: nc.alloc_semaphore("sem_" + e) for e in ("pe", "act", "dve", "pool")}
        self.dma_sems = {q: [nc.alloc_semaphore("dsem_%s_%d" % (q, i)) for i in range(ndma_sems)]
                         for q in ("sp", "pool")}
        self.dma_hist = {q: [] for q in ("sp", "pool")}
        self.stores = []

    def granules(self, ap):
        key = (ap.tensor.name, ap.offset, tuple(ap.ap), str(ap.dtype))
        g = self.gcache.get(key)
        if g is not None:
            return g
        sz = DTSZ[ap.dtype]
        dims = list(ap.ap)
        row = dims[0][0]
        off = ap.offset % row if row > 0 else ap.offset
        free = dims[1:]
        name = ap.tensor.name
        res = set()
        if not free:
            free = [(1, 1)]
        last = free[-1]
        outer = free[:-1]
        bases = [off]
        for (st, cnt) in outer:
            bases = [b + st * i for b in bases for i in range(cnt)]
        ls, lc = last
        span = 1 if ls == 0 else ls * (lc - 1) + 1
        for b in bases:
            lo = (b * sz) // GRAN
            hi = ((b + span) * sz - 1) // GRAN
            for gi in range(lo, hi + 1):
                res.add((name, gi))
        g = frozenset(res)
        self.gcache[key] = g
        return g

    def op(self, eng, fn, reads=(), writes=(), is_dma=False, store=False, cc=False):
        ins = Ins(eng, fn, is_dma)
        deps = ins.deps
        tr = self.track
        for ap in reads:
            for g in self.granules(ap):
                e = tr.get(g)
                if e is None:
                    tr[g] = [None, [ins]]
                else:
                    if e[0] is not None:
                        deps.add(e[0])
                    e[1].append(ins)
        for ap in writes:
            for g in self.granules(ap):
                e = tr.get(g)
                if e is None:
                    tr[g] = [ins, []]
                else:
                    if e[0] is not None:
                        deps.add(e[0])
                    for r in e[1]:
                        deps.add(r)
                    e[0] = ins
                    e[1] = []
        deps.discard(ins)
        if eng == "pe":
            ins.deps = deps = {d for d in deps if not (d.eng == "pe" and not d.is_dma)}
        for d in deps:
            d.sig = True
        if cc:
            ins.dsem = self.nc.alloc_semaphore("ccsem_%d" % len(self.lists[eng]))
            ins.dval = 1
            ins.cc = True
        elif is_dma:
            h = self.dma_hist[eng]
            j = len(h)
            sems = self.dma_sems[eng]
            k = len(sems)
            ins.dsem = sems[j % k]
            ins.dval = 16 * (j // k + 1)
            if j >= k:
                ins.dprev = h[j - k]
            h.append(ins)
            if store:
                self.stores.append(ins)
        self.lists[eng].append(ins)
        return ins

    def emit(self):
        nc = self.nc
        for e in self.ENGS:
            cnt = 0
            for ins in self.lists[e]:
                if not ins.is_dma and ins.sig:
                    cnt += 1
                    ins.val = cnt
        handles = {"pe": nc.tensor, "act": nc.scalar, "dve": nc.vector, "pool": nc.gpsimd, "sp": nc.sync}
        bnames = {"pe": "tensor", "act": "scalar", "dve": "vector", "pool": "gpsimd", "sp": "sync"}
        nwaits = {e: 0 for e in self.ENGS}

        def run_engine(e):
            h = handles[e]
            waited = {}

            def wait(sem, val):
                key = id(sem)
                if waited.get(key, 0) >= val:
                    return
                h.wait_ge(sem, val)
                waited[key] = val
                nwaits[e] += 1

            for ins in self.lists[e]:
                for d in ins.deps:
                    if d.is_dma:
                        wait(d.dsem, d.dval)
                    else:
                        wait(self.eng_sem[d.eng], d.val)
                if ins.is_dma and ins.dprev is not None:
                    wait(ins.dprev.dsem, ins.dprev.dval)
                inst = ins.fn()
                if ins.cc:
                    inst.then_inc(ins.dsem)
                elif ins.is_dma:
                    inst.then_inc(ins.dsem, 16)
                elif ins.sig:
                    inst.then_inc(self.eng_sem[e], 1)
            if e == "sp":
                for st in self.stores:
                    wait(st.dsem, st.dval)

        with nc.Block() as block:
            for e in self.ENGS:
                getattr(block, bnames[e])(lambda _h, _e=e: run_engine(_e))
        self.nwaits = nwaits


class Prog:
    def __init__(self, n_layers=NS, final_norm=True):
        self.n_layers = n_layers
        self.final_norm = final_norm
        nc = self.nc = bass.Bass("TRN2", target_bir_lowering=False)
        self.S = Sched(nc)
        L = NS
        dt = nc.dram_tensor

        def din(name, shape, dtype):
            return dt(name, list(shape), dtype, kind="ExternalInput").ap()

        def dout(name, shape, dtype):
            return dt(name, list(shape), dtype, kind="ExternalOutput").ap()

        self.d = {}
        for name, shape, dtype in [
            ("x", (NTOK, D), F32), ("pos", (128, NT), I32), ("invf", (128, 32), F32),
            ("norm_ffn1_t", (L, 128, 16), F32), ("norm_mix_t", (L, 128, 16), F32), ("norm_ffn2_t", (L, 128, 16), F32),
            ("norm_final", (1, D), F32),
            ("ffn1_wi", (L, D, 2 * DFF), F32), ("ffn1_wo", (L, DFF, D), F32),
            ("ffn2_wi", (L, D, 2 * DFF), F32), ("ffn2_wo", (L, DFF, D), F32),
            ("w_in", (L, D, WIN), F32), ("b_gate_t", (L, 128, 48), F32), ("att_sinks", (L, 12), F32),
            ("w_pool", (L, 4, 192, 192), F32), ("pool_scale_t", (L, 96, 8), F32),
            ("w_gla_a2", (L, 16, 384), F32), ("b_gla_a", (L, 384), F32), ("gla_norm", (L, 768), F32),
            ("w_br_att", (L, 768, D), F32), ("w_br_pool", (L, 768, D), F32), ("w_br_gla", (L, 768, D), F32),
            ("w_out", (L, D, D), F32),
            ("ident", (128, 128), BF16), ("maskA", (128, 256), F32), ("mask0", (128, 256), F32),
            ("tri_incl", (128, 128), BF16), ("tri_su", (128, 128), BF16), ("maskT", (128, 128), BF16),
            ("bands", (128, 8, 128), BF16), ("bands0", (128, 8, 128), BF16),
            ("czero", (128, CCW), F32), ("flag", (128, 1), F32),
        ]:
            self.d[name] = din(name, shape, dtype)
        self.o_y = dout("y", (NTOK, D), F32)
        self.cin = [dt("cin%d" % i, [128, CCW], F32).ap() for i in range(NS - 1)]
        self.cout = [dt("cout%d" % i, [256, CCW], F32).ap() for i in range(NS - 1)]

        self.arena_bytes = 212000
        self.A = nc.alloc_sbuf_tensor("arena", [128, self.arena_bytes], U8)
        self.top = 0
        al = self.alloc
        self.x = al(F32, NT, D)
        self.hT = al(BF16, 16, NTOK)
        self.ring = [al(BF16, 4096) for _ in range(4)]
        self.ring_i = 0
        self.ident = al(BF16, 128)
        self.maskA = al(F32, 256)
        self.mask0 = al(F32, 256)
        self.tri_incl = al(BF16, 128)
        self.tri_su = al(BF16, 128)
        self.maskT = al(BF16, 128)
        self.bands = al(BF16, 8, 128)
        self.bands0 = al(BF16, 8, 128)
        self.cos = al(F32, NT, 32)
        self.sin = al(F32, NT, 32)
        self.cst = al(F32, 8)
        self.stat = al(F32, 32)
        self.gains = al(F32, 3, 16)
        self.bgate = al(F32, 48)
        self.sinks = al(F32, 12)
        self.pscale = al(F32, 8, parts=96)
        self.ba_bc = al(F32, 384)
        self.gn_bc = al(F32, 768)
        self.w_a2 = al(BF16, 384, parts=16)
        self.w_pool = al(BF16, 8, 192, parts=96)
        self.KT = al(BF16, 4, 5, 128, parts=64)
        self.Vb = al(BF16, 5, 256)
        self.uh = al(BF16, 768)
        self.Sst = al(F32, 4, 192, parts=96)
        self.Sbf = al(BF16, 4, 192, parts=96)
        big_off = self.alloc_bytes(4 * 2320 * 2)
        self.p_tm = self.view(big_off, 128, BF16, 4, 2320)
        self.gT = self.view(big_off, 128, BF16, 8, NTOK)
        yt_off = self.alloc_bytes(8192)
        self.yT128 = self.view(yt_off, 128, BF16, 6, 512)
        self.yT96 = self.view(yt_off, 96, BF16, 8, 512)
        self.tmp_off = self.top
        self.tmp_top = self.top
        assert self.top <= self.arena_bytes, self.top
        self.banks = [nc.alloc_psum_tensor("ps%d" % i, [128, 512], F32) for i in range(8)]
        self.bank_i = 0

    def alloc_bytes(self, n, align=64):
        off = (self.top + align - 1) // align * align
        self.top = off + n
        assert self.top <= self.arena_bytes, ("SBUF overflow", self.top)
        return off

    def view(self, off, parts, dtype, *dims):
        n = int(np.prod(dims)) * DTSZ[dtype]
        ap = self.A[0:parts, off:off + n].bitcast(dtype)
        if len(dims) == 2:
            ap = ap.rearrange("p (a b) -> p a b", b=dims[1])
        elif len(dims) == 3:
            ap = ap.rearrange("p (a b c) -> p a b c", b=dims[1], c=dims[2])
        return ap

    def alloc(self, dtype, *dims, parts=128):
        off = self.alloc_bytes(int(np.prod(dims)) * DTSZ[dtype])
        return self.view(off, parts, dtype, *dims)

    def tmp_reset(self):
        self.tmp_top = self.tmp_off

    def tmp(self, dtype, *dims, parts=128):
        n = int(np.prod(dims)) * DTSZ[dtype]
        off = (self.tmp_top + 63) // 64 * 64
        self.tmp_top = off + n
        assert self.tmp_top <= self.arena_bytes, ("SBUF tmp overflow", self.tmp_top)
        return self.view(off, parts, dtype, *dims)

    def bank(self):
        b = self.banks[self.bank_i % 8]
        self.bank_i += 1
        return b

    def slot(self):
        s = self.ring[self.ring_i % 4]
        self.ring_i += 1
        return s

    def mm(self, out, lhsT, rhs, start=True, stop=True):
        nc = self.nc
        self.S.op("pe", lambda: nc.tensor.matmul(out, lhsT=lhsT, rhs=rhs, start=start, stop=stop),
                  reads=[lhsT, rhs], writes=[out])

    def tr(self, out, in_):
        nc = self.nc
        ident = self.ident
        self.S.op("pe", lambda: nc.tensor.transpose(out, in_, ident), reads=[in_, ident], writes=[out])

    def act(self, out, in_, func, bias=None, scale=None, accum=None):
        nc = self.nc
        kw = {}
        reads = [in_]
        if bias is not None:
            kw["bias"] = bias
            if not isinstance(bias, (int, float)):
                reads.append(bias)
        if scale is not None:
            kw["scale"] = scale
            if not isinstance(scale, (int, float)):
                reads.append(scale)
        writes = [out]
        if accum is not None:
            kw["accum_out"] = accum
            writes.append(accum)
        self.S.op("act", lambda: nc.scalar.activation(out=out, in_=in_, func=func, **kw), reads=reads, writes=writes)

    def tt(self, out, in0, in1, op, eng="dve"):
        nc = self.nc
        e = nc.vector if eng == "dve" else nc.gpsimd
        self.S.op(eng, lambda: e.tensor_tensor(out=out, in0=in0, in1=in1, op=op), reads=[in0, in1], writes=[out])

    def ts(self, out, in0, s1, op0, s2=None, op1=None, eng="dve"):
        nc = self.nc
        e = nc.vector if eng == "dve" else nc.gpsimd
        reads = [in0]
        if not isinstance(s1, (int, float)):
            reads.append(s1)
        if s2 is not None and not isinstance(s2, (int, float)):
            reads.append(s2)
        if op1 is None:
            f = lambda: e.tensor_scalar(out=out, in0=in0, scalar1=s1, scalar2=None, op0=op0)
        else:
            f = lambda: e.tensor_scalar(out=out, in0=in0, scalar1=s1, scalar2=s2, op0=op0, op1=op1)
        self.S.op(eng, f, reads=reads, writes=[out])

    def stt(self, out, in0, scalar, in1, op0, op1):
        nc = self.nc
        reads = [in0, in1]
        if not isinstance(scalar, (int, float)):
            reads.append(scalar)
        self.S.op("dve", lambda: nc.vector.scalar_tensor_tensor(out=out, in0=in0, scalar=scalar, in1=in1,
                                                                 op0=op0, op1=op1), reads=reads, writes=[out])

    def red(self, out, in_, op, axis=AX.X, negate=False):
        nc = self.nc
        if negate:
            f = lambda: nc.vector.tensor_reduce(out=out, in_=in_, axis=axis, op=op, negate=True)
        else:
            f = lambda: nc.vector.tensor_reduce(out=out, in_=in_, axis=axis, op=op)
        self.S.op("dve", f, reads=[in_], writes=[out])

    def cp(self, out, in_, eng="dve"):
        nc = self.nc
        if eng == "act":
            self.S.op("act", lambda: nc.scalar.copy(out=out, in_=in_), reads=[in_], writes=[out])
        else:
            e = nc.vector if eng == "dve" else nc.gpsimd
            self.S.op(eng, lambda: e.tensor_copy(out=out, in_=in_), reads=[in_], writes=[out])

    def recip(self, out, in_):
        nc = self.nc
        self.S.op("dve", lambda: nc.vector.reciprocal(out=out, in_=in_), reads=[in_], writes=[out])

    def memset(self, out, val):
        nc = self.nc
        self.S.op("dve", lambda: nc.vector.memset(out, val), reads=[], writes=[out])

    def load(self, out, in_, cast=False, tracked_src=False):
        nc = self.nc
        rd = [in_] if tracked_src else []
        if cast:
            self.S.op("pool", lambda: nc.gpsimd.dma_start(out=out, in_=in_), reads=rd, writes=[out], is_dma=True)
        else:
            self.S.op("sp", lambda: nc.sync.dma_start(out=out, in_=in_), reads=rd, writes=[out], is_dma=True)

    def dstore(self, out, in_):
        nc = self.nc
        self.S.op("sp", lambda: nc.sync.dma_start(out=out, in_=in_), reads=[in_], writes=[out], is_dma=True)

    def exchange(self, l):
        nc = self.nc
        if getattr(self, "no_exchange", False):
            return
        cin, cout = self.cin[l], self.cout[l]
        self.S.op("pool", lambda: nc.gpsimd.collective_compute(
            "AllGather", ALU.bypass, replica_groups=[[0, 1], [2, 3], [4, 5], [6, 7]],
            ins=[cin.opt()], outs=[cout.opt()]), reads=[cin], writes=[cout], is_dma=True, cc=True)

    def carry_src(self, l):
        if l == 0:
            return self.d["czero"], False
        return self.cout[l - 1][0:128, :], True

    @staticmethod
    def pbc(row):
        return row.partition_broadcast(128)[:, 0, :]

    def store(self, out, in_):
        nc = self.nc
        self.S.op("sp", lambda: nc.sync.dma_start(out=out, in_=in_), reads=[in_], writes=[], is_dma=True, store=True)

    def wblock(self, w2d, c0, ncols, kparts=128, r0=0, nk=16):
        s = self.slot()
        v = s[0:kparts, 0:nk * ncols].rearrange("p (k n) -> p k n", n=ncols)
        src = w2d[r0:r0 + nk * kparts, c0:c0 + ncols].rearrange("(k p) n -> p k n", p=kparts)
        self.load(v, src, cast=True)
        return v

    def setup(self):
        d = self.d
        xv = d["x"].rearrange("(t p) d -> p t d", p=128)
        for t in range(NT):
            self.load(self.x[:, t, :], xv[:, t, :])
        for name, dst in [("ident", self.ident), ("maskA", self.maskA), ("mask0", self.mask0),
                          ("tri_incl", self.tri_incl), ("tri_su", self.tri_su), ("maskT", self.maskT),
                          ("bands", self.bands), ("bands0", self.bands0)]:
            self.load(dst, d[name])
        self.memset(self.cst[:, 0:1], EPS)
        self.memset(self.cst[:, 1:2], 1.0)
        self.memset(self.cst[:, 2:3], 0.0)
        self.load(self.cst[:, 3:4], d["flag"])
        self.tmp_reset()
        posi = self.tmp(I32, NT)
        invf = self.tmp(F32, 32)
        posf = self.tmp(F32, NT)
        ang = self.tmp(F32, NT, 32)
        a2 = self.tmp(F32, NT, 32)
        kq = self.tmp(F32, NT, 32)
        ki = self.tmp(I32, NT, 32)
        kf = self.tmp(F32, NT, 32)
        m = self.tmp(F32, NT, 32)
        self.load(posi, d["pos"])
        self.load(invf, d["invf"])
        self.cp(posf, posi)
        self.tt(ang, posf.unsqueeze(2).broadcast_to([128, NT, 32]), invf.unsqueeze(1).broadcast_to([128, NT, 32]), ALU.mult)
        TWO_PI = 2.0 * np.pi
        C1 = 6.28125
        C2 = TWO_PI - C1
        for shift, dst in [(0.0, self.sin), (np.pi / 2, self.cos)]:
            if shift != 0.0:
                self.ts(a2, ang, float(shift), ALU.add)
                a = a2
            else:
                a = ang
            self.ts(kq, a, float(1.0 / TWO_PI), ALU.mult)
            self.cp(ki, kq)
            self.cp(kf, ki)
            self.stt(m, kf, float(-C1), a, ALU.mult, ALU.add)
            self.stt(m, kf, float(-C2), m, ALU.mult, ALU.add)
            self.ts(kq, m, float(np.pi), ALU.is_gt, float(-TWO_PI), ALU.mult)
            self.tt(m, m, kq, ALU.add)
            self.ts(kq, m, float(-np.pi), ALU.is_lt, float(TWO_PI), ALU.mult)
            self.tt(m, m, kq, ALU.add)
            self.ts(m, m, 3.1415925, ALU.min, -3.1415925, ALU.max)
            self.act(dst, m, AF.Sin)

    def layer_params(self, l):
        d = self.d
        self.load(self.gains[:, 0, :], d["norm_ffn1_t"][l])
        self.load(self.gains[:, 1, :], d["norm_mix_t"][l])
        self.load(self.gains[:, 2, :], d["norm_ffn2_t"][l])
        self.load(self.bgate, d["b_gate_t"][l])
        self.load(self.sinks, self.pbc(d["att_sinks"][l:l + 1, :]))
        self.load(self.pscale, d["pool_scale_t"][l])
        self.load(self.ba_bc, self.pbc(d["b_gla_a"][l:l + 1, :]))
        self.load(self.gn_bc, self.pbc(d["gla_norm"][l:l + 1, :]))
        self.load(self.w_a2, d["w_gla_a2"][l], cast=True)
        self.load(self.w_pool, d["w_pool"][l].rearrange("g (i p) d -> p (g i) d", p=96), cast=True)

    def rstd_of(self, xt, st):
        junk = self.tmp(BF16, D)
        self.act(junk, xt, AF.Square, accum=st[:, 0:1])
        self.act(st[:, 1:2], st[:, 0:1], AF.Ln, scale=1.0 / D, bias=self.cst[:, 0:1])
        self.act(st[:, 2:3], st[:, 1:2], AF.Exp, scale=-0.5)

    def norm_T(self, tiles, gi, col0):
        self.tmp_reset()
        sts = [self.tmp(F32, 4) for _ in range(2)]
        xss = [self.tmp(BF16, D) for _ in range(2)]
        gain = self.gains[:, gi, :]
        for j, t in enumerate(tiles):
            st = sts[j % 2]
            xs = xss[j % 2]
            save = self.tmp_top
            self.rstd_of(self.x[:, t, :], st)
            self.tmp_top = save
            self.ts(xs, self.x[:, t, :], st[:, 2:3], ALU.mult)
            for hb in range(2):
                pb = self.bank()[:, 0:512].bitcast(BF16).rearrange("p (k n) -> p k n", n=128)
                for k in range(8):
                    kc = hb * 8 + k
                    self.tr(pb[:, k, :], xs[:, kc * 128:(kc + 1) * 128])
                c = col0 + j * 128
                self.tt(self.hT[:, hb * 8:(hb + 1) * 8, c:c + 128], pb,
                        gain[:, hb * 8:(hb + 1) * 8].unsqueeze(2).broadcast_to([128, 8, 128]), ALU.mult)

    def ffn(self, wi, wo):
        self.tmp_reset()
        sil = [self.tmp(F32, 512) for _ in range(2)]
        si = 0
        rounds = [(c0, min(8, NCH - c0)) for c0 in range(0, NCH, 8)]
        for (c0, nchk) in rounds:
            for cp2 in range(0, nchk, 2):
                ch0 = c0 + cp2
                wa = self.wblock(wi, ch0 * 128, 256)
                wb = self.wblock(wi, DFF + ch0 * 128, 256)
                for cc in range(2):
                    lc = cp2 + cc
                    for th in range(NTOK // 512):
                        pa = self.bank()
                        pb = self.bank()
                        for kc in range(16):
                            self.mm(pa[:, :], wa[:, kc, cc * 128:(cc + 1) * 128], self.hT[:, kc, th * 512:(th + 1) * 512],
                                    start=(kc == 0), stop=(kc == 15))
                        for kc in range(16):
                            self.mm(pb[:, :], wb[:, kc, cc * 128:(cc + 1) * 128], self.hT[:, kc, th * 512:(th + 1) * 512],
                                    start=(kc == 0), stop=(kc == 15))
                        s = sil[si % 2]
                        si += 1
                        self.act(s, pa[:, :], AF.Silu)
                        self.tt(self.gT[:, lc, th * 512:(th + 1) * 512], s, pb[:, :], ALU.mult)
            for dg in range(4):
                wv = self.wblock(wo, dg * 512, 512, r0=c0 * 128, nk=nchk)
                for t in range(NT):
                    po = self.bank()
                    for lc in range(nchk):
                        self.mm(po[:, :], self.gT[:, lc, t * 128:(t + 1) * 128], wv[:, lc, :],
                                start=(lc == 0), stop=(lc == nchk - 1))
                    xd = self.x[:, t, dg * 512:(dg + 1) * 512]
                    self.stt(xd, po[:, :], 0.5, xd, ALU.mult, ALU.add)

    def project(self, w_in, hf, c_start, c_end, base_col, evac):
        c = c_start
        while c < c_end:
            n = min(256, c_end - c)
            wv = self.wblock(w_in, c, n)
            for j in range(4):
                ps = self.bank()
                for kc in range(16):
                    self.mm(ps[:, 0:n], self.hT[:, kc, j * 128:(j + 1) * 128], wv[:, kc, :],
                            start=(kc == 0), stop=(kc == 15))
                evac(j, ps[:, 0:n], c - base_col, c - base_col + n)
            c += n

    def merge_branch(self, l, bi, w_in, w_br, yT, nyc, first):
        self.tmp_reset()
        mT = self.hT[:, :, 512:1024]
        sig = [self.tmp(F32, 512) for _ in range(2)]
        tmpz = [self.tmp(F32, 512) for _ in range(2)]
        kp = yT.shape[0]
        i = 0
        for ng in range(4):
            wgs = [self.wblock(w_in, C_GATE + bi * D + ng * 512 + q * 256, 256) for q in range(2)]
            wbr = self.wblock(w_br, ng * 512, 512, kparts=kp, nk=nyc)
            for nn in range(4):
                n = ng * 4 + nn
                wg = wgs[nn // 2]
                cc = (nn % 2) * 128
                pg = self.bank()
                pz = self.bank()
                for kc in range(16):
                    self.mm(pg[:, :], wg[:, kc, cc:cc + 128], self.hT[:, kc, 0:512], start=(kc == 0), stop=(kc == 15))
                for c in range(nyc):
                    self.mm(pz[:, :], wbr[:, c, nn * 128:(nn + 1) * 128], yT[:, c, :], start=(c == 0), stop=(c == nyc - 1))
                sg = sig[i % 2]
                tz = tmpz[i % 2]
                i += 1
                self.act(sg, pg[:, :], AF.Sigmoid, bias=self.bgate[:, bi * 16 + n:bi * 16 + n + 1])
                if first:
                    self.tt(mT[:, n, :], sg, pz[:, :], ALU.mult)
                else:
                    self.tt(tz, sg, pz[:, :], ALU.mult)
                    self.tt(mT[:, n, :], tz, mT[:, n, :], ALU.add)

    def attention(self, l, hf, w_in):
        self.tmp_reset()
        p_tm = self.p_tm
        cosv, sinv = self.cos, self.sin

        def evac(j, ps, lo, hi):
            t = hf * 4 + j
            if lo >= C_V:
                self.cp(self.Vb[:, j + 1, lo - C_V:hi - C_V], ps, eng="act")
                return
            nh = (hi - lo) // 64
            src = ps.rearrange("p (h two f) -> p h two f", two=2, f=32)
            dst = p_tm[:, j, lo:hi].rearrange("p (h two f) -> p h two f", two=2, f=32)
            cb = cosv[:, t, :].unsqueeze(1).broadcast_to([128, nh, 32])
            sb = sinv[:, t, :].unsqueeze(1).broadcast_to([128, nh, 32])
            t1 = self.rt1[:, 0:nh, :]
            t2 = self.rt2[:, 0:nh, :]
            x1 = src[:, :, 0, :]
            x2 = src[:, :, 1, :]
            self.tt(t1, x1, cb, ALU.mult)
            self.tt(t2, x2, sb, ALU.mult)
            self.tt(dst[:, :, 0, :], t1, t2, ALU.subtract)
            self.tt(t1, x2, cb, ALU.mult)
            self.tt(t2, x1, sb, ALU.mult)
            self.tt(dst[:, :, 1, :], t1, t2, ALU.add)

        self.rt1 = self.tmp(F32, 4, 32)
        self.rt2 = self.tmp(F32, 4, 32)
        if hf == 0:
            src, trk = self.carry_src(l)
            self.load(self.KT[:, :, 0, :], src[0:64, 0:256].bitcast(BF16).rearrange("p (h n) -> p h n", n=128), tracked_src=trk)
            self.load(self.Vb[:, 0, :], src[:, 256:384].bitcast(BF16), tracked_src=trk)
        else:
            self.cp(self.KT[:, :, 0, :], self.KT[:, :, 4, :], eng="act")
            self.cp(self.Vb[:, 0, :], self.Vb[:, 4, :], eng="act")
        self.project(w_in, hf, 0, C_PU, 0, evac)
        QT = [self.tmp(BF16, 12, 128, parts=64) for _ in range(2)]
        sm = [self.tmp(F32, 2, 258) for _ in range(2)]
        ee = [self.tmp(BF16, 2, 258) for _ in range(2)]
        PT = [self.tmp(BF16, 4, 128) for _ in range(2)]
        yat = [self.tmp(BF16, 768) for _ in range(2)]
        stt_ = [self.tmp(F32, 16) for _ in range(2)]
        pi = 0
        for j in range(4):
            t = hf * 4 + j
            qt = QT[j % 2]
            ya = yat[j % 2]
            pk = self.bank()[0:64, 0:256].bitcast(BF16).rearrange("p (h n) -> p h n", n=128)
            for kh in range(4):
                self.tr(pk[:, kh, :], p_tm[:, j, C_K + kh * 64:C_K + (kh + 1) * 64])
            self.cp(self.KT[:, :, j + 1, :], pk, eng="act")
            for qb in range(2):
                pq = self.bank()[0:64, 0:384].bitcast(BF16).rearrange("p (h n) -> p h n", n=128)
                for hh in range(6):
                    h = qb * 6 + hh
                    self.tr(pq[:, hh, :], p_tm[:, j, h * 64:(h + 1) * 64])
                self.cp(qt[:, qb * 6:(qb + 1) * 6, :], pq, eng="act")
            mask = self.mask0 if (hf == 0 and j == 0) else self.maskA
            for hp in range(6):
                s_m = sm[pi % 2]
                e_ = ee[pi % 2]
                pt = PT[pi % 2]
                st = stt_[pi % 2]
                pi += 1
                ps = self.bank()
                psv = ps[:, :].rearrange("p (a n) -> p a n", n=256)
                for a in range(2):
                    h = hp * 2 + a
                    kh = h // 3
                    self.mm(psv[:, a, :], qt[:, h, :], self.KT[:, kh, j:j + 2, :].rearrange("p a n -> p (a n)"), start=True, stop=True)
                self.cp(s_m[:, :, 256:257], self.sinks[:, hp * 2:hp * 2 + 2].unsqueeze(2))
                self.stt(s_m[:, :, 0:256], psv, 0.125, mask.unsqueeze(1).broadcast_to([128, 2, 256]), ALU.mult, ALU.add)
                self.red(st[:, 2:4], s_m[:, :, 0:257], ALU.max, negate=True)
                for a in range(2):
                    self.act(e_[:, a, 0:257], s_m[:, a, 0:257], AF.Exp, bias=st[:, 2 + a:3 + a], accum=st[:, 4 + a:5 + a])
                self.recip(st[:, 12:14], st[:, 4:6])
                pp = self.bank()[:, 0:256].bitcast(BF16).rearrange("p (a n) -> p a n", n=128)
                for a in range(2):
                    for kb in range(2):
                        self.tr(pp[:, a * 2 + kb, :], e_[:, a, kb * 128:(kb + 1) * 128])
                self.cp(pt, pp, eng="act")
                po = self.bank()
                for a in range(2):
                    h = hp * 2 + a
                    kh = h // 3
                    for kb in range(2):
                        self.mm(po[:, a * 64:(a + 1) * 64], pt[:, a * 2 + kb, :],
                                self.Vb[:, j + kb, kh * 64:(kh + 1) * 64], start=(kb == 0), stop=(kb == 1))
                for a in range(2):
                    h = hp * 2 + a
                    self.ts(ya[:, h * 64:(h + 1) * 64], po[:, a * 64:(a + 1) * 64], st[:, 12 + a:13 + a], ALU.mult)
            py = self.bank()[:, 0:384].bitcast(BF16).rearrange("p (c n) -> p c n", n=128)
            for c in range(6):
                self.tr(py[:, c, :], ya[:, c * 128:(c + 1) * 128])
            self.cp(self.yT128[:, :, j * 128:(j + 1) * 128], py, eng="act")
        if hf == 1 and l < NS - 1:
            self.dstore(self.cin[l][0:64, 0:256].bitcast(BF16).rearrange("p (h n) -> p h n", n=128), self.KT[:, :, 4, :])
            self.dstore(self.cin[l][:, 256:384].bitcast(BF16), self.Vb[:, 4, :])

    def pool(self, l, hf, w_in):
        self.tmp_reset()
        p_tm = self.p_tm

        def evac(j, ps, lo, hi):
            self.cp(p_tm[:, j, lo:hi], ps, eng="act")

        if hf == 0:
            src, trk = self.carry_src(l)
            self.load(self.uh, src[:, 384:768].bitcast(BF16), tracked_src=trk)
        self.project(w_in, hf, C_PU, C_GQ, C_PU, evac)
        dTs = [self.tmp(BF16, 8, 128, parts=96) for _ in range(2)]
        for j in range(4):
            dT = dTs[j % 2]
            first = (hf == 0 and j == 0)
            bands = self.bands0 if first else self.bands
            prev = self.uh if j == 0 else p_tm[:, j - 1, 0:768]
            cur = p_tm[:, j, 0:768]
            for half in range(2):
                pd = self.bank()[0:96, :].rearrange("p (c n) -> p c n", n=128)
                for cc in range(4):
                    c = half * 4 + cc
                    g = c // 2
                    self.mm(pd[:, cc, :], cur[:, c * 96:(c + 1) * 96], bands[:, g, :], start=True, stop=False)
                    self.mm(pd[:, cc, :], prev[:, c * 96:(c + 1) * 96], bands[:, 4 + g, :], start=False, stop=True)
                self.cp(dT[:, half * 4:(half + 1) * 4, :], pd, eng="act")
            for half in range(2):
                py = self.bank()[0:96, :].rearrange("p (c n) -> p c n", n=128)
                for cc in range(4):
                    c = half * 4 + cc
                    g, oc = c // 2, c % 2
                    for ic in range(2):
                        self.mm(py[:, cc, :], self.w_pool[:, g * 2 + ic, oc * 96:(oc + 1) * 96], dT[:, g * 2 + ic, :],
                                start=(ic == 0), stop=(ic == 1))
                self.tt(self.yT96[:, half * 4:(half + 1) * 4, j * 128:(j + 1) * 128], py,
                        self.pscale[:, half * 4:(half + 1) * 4].unsqueeze(2).broadcast_to([96, 4, 128]), ALU.mult)
        self.cp(self.uh, p_tm[:, 3, 0:768], eng="act")
        if hf == 1 and l < NS - 1:
            self.dstore(self.cin[l][:, 384:768].bitcast(BF16), self.uh)

    def gla(self, l, hf, w_in):
        self.tmp_reset()
        p_tm = self.p_tm
        G0 = C_GQ

        def evac(j, ps, lo, hi):
            og_lo, og_hi = C_GO - G0, C_LR - G0
            a, b = max(lo, og_lo), min(hi, og_hi)
            if a < b:
                self.act(p_tm[:, j, a:b], ps[:, a - lo:b - lo], AF.Silu)
            if lo < og_lo:
                b2 = min(hi, og_lo)
                self.cp(p_tm[:, j, lo:b2], ps[:, 0:b2 - lo], eng="act")
            if hi > og_hi:
                a2 = max(lo, og_hi)
                self.cp(p_tm[:, j, a2:hi], ps[:, a2 - lo:hi - lo], eng="act")

        if hf == 0:
            src, trk = self.carry_src(l)
            self.load(self.Sst, src[0:96, 768:1536].rearrange("p (h v) -> p h v", v=192), tracked_src=trk)
            self.ts(self.Sst, self.Sst, self.cst[0:96, 3:4], ALU.mult)
            self.cp(self.Sbf, self.Sst, eng="act")
        self.project(w_in, hf, C_GQ, C_GATE, C_GQ, evac)
        Q0, K0, V0, O0, R0 = 0, C_GK - G0, C_GV - G0, C_GO - G0, C_LR - G0
        lrT = self.tmp(BF16, 128, parts=16)
        zb = self.tmp(F32, 384)
        ax = self.tmp(F32, 384)
        mn = self.tmp(F32, 384)
        gk = self.tmp(BF16, 384)
        eb = self.tmp(F32, 384)
        enb = self.tmp(F32, 384)
        er = self.tmp(F32, 384)
        qt = self.tmp(BF16, 384)
        kt = self.tmp(BF16, 384)
        ks = self.tmp(BF16, 384)
        qkT = self.tmp(BF16, 8, 128, parts=96)
        ATm = self.tmp(BF16, 4, 128)
        dec = self.tmp(F32, 4, parts=96)
        st = self.tmp(F32, 16)
        otmp = self.tmp(F32, 768)
        yg = self.tmp(BF16, 768)
        onesc = self.tri_incl[:, 127:128]
        for j in range(4):
            pt = self.bank()[0:16, 0:64].bitcast(BF16)
            self.tr(pt, p_tm[:, j, R0:R0 + 16])
            self.cp(lrT, pt, eng="act")
            pz = self.bank()
            self.mm(pz[:, 0:384], lrT, self.w_a2, start=True, stop=True)
            self.tt(zb, pz[:, 0:384], self.ba_bc, ALU.add)
            self.stt(ax, zb, -1.0, zb, ALU.mult, ALU.max)
            self.act(ax, ax, AF.Exp, scale=-1.0)
            self.act(ax, ax, AF.Ln, bias=self.cst[:, 1:2])
            self.ts(mn, zb, 0.0, ALU.min)
            self.stt(mn, ax, -1.0, mn, ALU.mult, ALU.add)
            self.ts(gk, mn, 1.0 / 16.0, ALU.mult)
            pb_ = self.bank()
            pr_ = self.bank()
            self.mm(pb_[:, 0:384], self.tri_incl, gk, start=True, stop=True)
            self.mm(pr_[:, 0:384], self.tri_su, gk, start=True, stop=True)
            pd_ = self.bank()
            for h in range(4):
                self.mm(pd_[0:96, h:h + 1], gk[:, h * 96:(h + 1) * 96], onesc, start=True, stop=True)
            self.act(eb, pb_[:, 0:384], AF.Exp)
            self.act(enb, pb_[:, 0:384], AF.Exp, scale=-1.0)
            self.act(er, pr_[:, 0:384], AF.Exp)
            self.act(dec, pd_[0:96, 0:4], AF.Exp)
            self.stt(qt, p_tm[:, j, Q0:Q0 + 384], float(96 ** -0.5), eb, ALU.mult, ALU.mult)
            self.tt(kt, p_tm[:, j, K0:K0 + 384], enb, ALU.mult)
            self.tt(ks, p_tm[:, j, K0:K0 + 384], er, ALU.mult)
            pq = self.bank()[0:96, :].bitcast(BF16).rearrange("p (c n) -> p c n", n=128)
            for h in range(4):
                self.tr(pq[:, h, :], qt[:, h * 96:(h + 1) * 96])
                self.tr(pq[:, 4 + h, :], kt[:, h * 96:(h + 1) * 96])
            self.cp(qkT, pq[:, 0:8, :], eng="act")
            pa = self.bank()[:, :].rearrange("p (h n) -> p h n", n=128)
            for h in range(4):
                self.mm(pa[:, h, :], qkT[:, 4 + h, :], qkT[:, h, :], start=True, stop=True)
            self.tt(ATm, pa, self.maskT.unsqueeze(1).broadcast_to([128, 4, 128]), ALU.mult)
            po = [self.bank(), self.bank()]
            for h in range(4):
                o = po[h // 2][:, (h % 2) * 192:(h % 2 + 1) * 192]
                self.mm(o, ATm[:, h, :], p_tm[:, j, V0 + h * 192:V0 + (h + 1) * 192], start=True, stop=False)
                self.mm(o, qkT[:, h, :], self.Sbf[:, h, :], start=False, stop=True)
            pS = [self.bank(), self.bank()]
            for h in range(4):
                dS = pS[h // 2][0:96, (h % 2) * 192:(h % 2 + 1) * 192]
                self.mm(dS, ks[:, h * 96:(h + 1) * 96], p_tm[:, j, V0 + h * 192:V0 + (h + 1) * 192], start=True, stop=True)
            for h in range(4):
                dS = pS[h // 2][0:96, (h % 2) * 192:(h % 2 + 1) * 192]
                self.stt(self.Sst[:, h, :], self.Sst[:, h, :], dec[:, h:h + 1], dS, ALU.mult, ALU.add)
            self.cp(self.Sbf, self.Sst, eng="act")
            for h in range(4):
                o = po[h // 2][:, (h % 2) * 192:(h % 2 + 1) * 192]
                self.act(otmp[:, h * 192:(h + 1) * 192], o, AF.Square, accum=st[:, h:h + 1])
            self.act(st[:, 4:8], st[:, 0:4], AF.Ln, scale=1.0 / 192.0, bias=self.cst[:, 0:1])
            self.act(st[:, 8:12], st[:, 4:8], AF.Exp, scale=-0.5)
            for h in range(4):
                o = po[h // 2][:, (h % 2) * 192:(h % 2 + 1) * 192]
                self.stt(otmp[:, h * 192:(h + 1) * 192], o, st[:, 8 + h:9 + h], self.gn_bc[:, h * 192:(h + 1) * 192],
                         ALU.mult, ALU.mult)
            self.tt(yg, otmp, p_tm[:, j, O0:O0 + 768], ALU.mult)
            py = self.bank()[0:96, :].bitcast(BF16).rearrange("p (c n) -> p c n", n=128)
            for c in range(8):
                self.tr(py[:, c, :], yg[:, c * 96:(c + 1) * 96])
            self.cp(self.yT96[:, :, j * 128:(j + 1) * 128], py[:, 0:8, :], eng="act")
        if hf == 1 and l < NS - 1:
            self.dstore(self.cin[l][0:96, 768:1536].rearrange("p (h v) -> p h v", v=192), self.Sst)
            self.exchange(l)

    def mixer_half(self, l, hf):
        d = self.d
        w_in = d["w_in"][l]
        tiles = list(range(hf * 4, hf * 4 + 4))
        self.norm_T(tiles, 1, 0)
        self.attention(l, hf, w_in)
        self.merge_branch(l, 0, w_in, d["w_br_att"][l], self.yT128, 6, first=True)
        self.pool(l, hf, w_in)
        self.merge_branch(l, 1, w_in, d["w_br_pool"][l], self.yT96, 8, first=False)
        self.gla(l, hf, w_in)
        self.merge_branch(l, 2, w_in, d["w_br_gla"][l], self.yT96, 8, first=False)
        mT = self.hT[:, :, 512:1024]
        for dg in range(8):
            wv = self.wblock(d["w_out"][l], dg * 256, 256)
            for j in range(4):
                t = hf * 4 + j
                po = self.bank()
                for kc in range(16):
                    self.mm(po[:, 0:256], mT[:, kc, j * 128:(j + 1) * 128], wv[:, kc, :], start=(kc == 0), stop=(kc == 15))
                xd = self.x[:, t, dg * 256:(dg + 1) * 256]
                self.tt(xd, xd, po[:, 0:256], ALU.add)

    def final(self):
        self.tmp_reset()
        g_bc = self.ring[0][:, 0:4096].bitcast(F32)
        self.load(g_bc, self.pbc(self.d["norm_final"]))
        outs = [self.ring[1][:, 0:4096].bitcast(F32), self.ring[2][:, 0:4096].bitcast(F32)]
        sts = [self.tmp(F32, 4) for _ in range(2)]
        yv = self.o_y.rearrange("(t p) d -> p t d", p=128)
        for t in range(NT):
            st = sts[t % 2]
            o = outs[t % 2]
            save = self.tmp_top
            self.rstd_of(self.x[:, t, :], st)
            self.tmp_top = save
            self.stt(o, self.x[:, t, :], st[:, 2:3], g_bc, ALU.mult, ALU.mult)
            self.store(yv[:, t, :], o)

    def raw_out(self):
        yv = self.o_y.rearrange("(t p) d -> p t d", p=128)
        for t in range(NT):
            self.store(yv[:, t, :], self.x[:, t, :])

    def build(self):
        self.setup()
        d = self.d
        for l in range(self.n_layers):
            self.layer_params(l)
            self.norm_T(list(range(NT)), 0, 0)
            self.ffn(d["ffn1_wi"][l], d["ffn1_wo"][l])
            for hf in range(2):
                self.mixer_half(l, hf)
            self.norm_T(list(range(NT)), 2, 0)
            self.ffn(d["ffn2_wi"][l], d["ffn2_wo"][l])
        if self.final_norm:
            self.final()
        else:
            self.raw_out()
        self.S.emit()
        return self.nc


def _consts(first_half):
    bf = ml_dtypes.bfloat16
    c = {}
    c["ident"] = np.eye(128, dtype=np.float32).astype(bf)
    qi = np.arange(128)[:, None]
    si = np.arange(256)[None, :]
    rel = 128 + qi - si
    band = (rel >= 0) & (rel < 128)
    maskA = np.where(band, 0.0, NEG).astype(np.float32)
    c["maskA"] = maskA
    if first_half:
        c["mask0"] = np.where(band & (si >= 128), 0.0, NEG).astype(np.float32)
    else:
        c["mask0"] = maskA.copy()
    tp = np.arange(128)[:, None]
    tt = np.arange(128)[None, :]
    c["tri_incl"] = (tp <= tt).astype(np.float32).astype(bf)
    c["tri_su"] = (tp > tt).astype(np.float32).astype(bf)
    c["maskT"] = (tp <= tt).astype(np.float32).astype(bf)
    bands = np.zeros((128, 8, 128), np.float32)
    bands0 = np.zeros((128, 8, 128), np.float32)
    for g, w in enumerate((2, 4, 8, 16)):
        cur = ((tp <= tt) & (tp > tt - w)).astype(np.float32) / w - (tp == tt).astype(np.float32)
        prv = ((tp - 128) > (tt - w)).astype(np.float32) / w
        bands[:, g, :] = cur
        bands[:, 4 + g, :] = prv
        if first_half:
            cnt = np.minimum(tt + 1, w).astype(np.float32)
            bands0[:, g, :] = ((tp <= tt) & (tp > tt - w)).astype(np.float32) / cnt - (tp == tt).astype(np.float32)
        else:
            bands0[:, g, :] = cur
            bands0[:, 4 + g, :] = prv
    c["bands"] = bands.astype(bf)
    c["bands0"] = bands0.astype(bf)
    half = 32
    inv_freq = (10000.0 ** (-np.arange(half, dtype=np.float32) / half)).astype(np.float32)
    c["invf"] = np.broadcast_to(inv_freq[None, :], (128, 32)).copy()
    return c


_PROG_CACHE = {}


def _get_prog(n_layers=NS, final_norm=True):
    key = (n_layers, final_norm)
    if key not in _PROG_CACHE:
        _PROG_CACHE[key] = Prog(n_layers, final_norm).build()
    return _PROG_CACHE[key]


def _slotted(a, role):
    a = np.asarray(a, np.float32)
    out = np.zeros((NS,) + a.shape[1:], np.float32)
    out[role:role + L_ALL] = a
    return out


def _role_inputs(inp, role):
    f = np.float32
    sh = {}
    for k in ("ffn1_wi", "ffn1_wo", "ffn2_wi", "ffn2_wo", "w_in", "att_sinks", "w_pool", "w_gla_a2", "b_gla_a",
              "gla_norm", "w_br_att", "w_br_pool", "w_br_gla", "w_out"):
        sh[k] = _slotted(inp[k], role)
    for k in ("norm_ffn1", "norm_mix", "norm_ffn2"):
        sh[k + "_t"] = _slotted(np.asarray(inp[k], f).reshape(L_ALL, 16, 128).transpose(0, 2, 1), role)
    sh["norm_final"] = np.asarray(inp["norm_final"], f).reshape(1, D)
    sh["b_gate_t"] = _slotted(np.asarray(inp["b_gate"], f).reshape(L_ALL, 48, 128).transpose(0, 2, 1), role)
    sh["pool_scale_t"] = _slotted(np.asarray(inp["pool_scale"], f).reshape(L_ALL, 8, 96).transpose(0, 2, 1), role)
    sh.update(_consts(role == 0))
    sh["czero"] = np.zeros((128, CCW), f)
    sh["flag"] = np.full((128, 1), float(role), f)
    return sh


def _core_inputs(inp, sh, b, half):
    m = dict(sh)
    s0 = half * NTOK
    m["x"] = np.ascontiguousarray(np.asarray(inp["x"], np.float32)[b, s0:s0 + NTOK])
    pos = np.asarray(inp["positions"], np.int32)[b, s0:s0 + NTOK]
    m["pos"] = np.ascontiguousarray(pos.reshape(NT, 128).T)
    return m


def kernel(**inp):
    nc = _get_prog()
    roles = [_role_inputs(inp, 0), _role_inputs(inp, 1)]
    B = inp["x"].shape[0]
    maps = [_core_inputs(inp, roles[c % 2], c // 2, c % 2) for c in range(2 * B)]
    res = run_bass_kernel_spmd(nc, maps, core_ids=list(range(2 * B))).results
    out = np.zeros((B, 2 * NTOK, D), np.float32)
    for c in range(2 * B):
        out[c // 2, (c % 2) * NTOK:(c % 2 + 1) * NTOK] = res[c]["y"]
    return out
```
